# Optimizing a Trainium2 kernel written in Bass

```python
import math
import jax, jax.numpy as jnp
from jax import lax
import numpy as np

D_MODEL = 2048
BATCH = 1
SEQ = 16384
DEPTH = 1

N_MEM = 256
D_MIX = D_MODEL
D_CONV = 3 * D_MIX // 8
D_POOL = 3 * D_MIX // 8
D_XATT = D_MIX // 4
CONV_HEAD_DIM = 128
CONV_HEADS = D_CONV // CONV_HEAD_DIM
CONV_WIDTH = 3
POOL_WINDOWS = (2, 4, 8, 16)
POOL_GROUPS = len(POOL_WINDOWS)
POOL_GROUP_DIM = D_POOL // POOL_GROUPS
XATT_HEADS = 4
XATT_HEAD_DIM = D_XATT // XATT_HEADS
D_FF = ((8 * D_MODEL // 3 + 255) // 256) * 256
FFN_CONV_WIDTH = 3
D_IN = 3 * D_CONV + D_POOL + D_XATT
EPS = 1e-6

kernel_name = "hybrid_conv_pool_memxattn_block"


def rmsnorm(x, g):
    xf = x.astype(jnp.float32)
    y = xf * lax.rsqrt(jnp.mean(xf * xf, axis=-1, keepdims=True) + EPS)
    return (y * g.astype(jnp.float32)).astype(x.dtype)


def causal_dwconv(u, w):
    k = w.shape[0]
    s = u.shape[1]
    up = jnp.pad(u, ((0, 0), (k - 1, 0), (0, 0)))
    out = w[k - 1] * u
    for j in range(k - 1):
        out = out + w[j] * up[:, j:j + s]
    return out


def causal_pool_minus_self(v):
    s = v.shape[1]
    vf = v.astype(jnp.float32)
    c = jnp.pad(jnp.cumsum(vf, axis=1), ((0, 0), (1, 0), (0, 0), (0, 0)))
    t = jnp.arange(s, dtype=jnp.float32)[None, :, None]
    outs = []
    for g, k in enumerate(POOL_WINDOWS):
        cg = c[:, :, g]
        hi = cg[:, 1:]
        lo = jnp.pad(cg[:, :s + 1 - k], ((0, 0), (k - 1, 0), (0, 0)))
        cnt = jnp.minimum(t + 1.0, float(k))
        outs.append((hi - lo) / cnt - vf[:, :, g])
    return jnp.stack(outs, axis=2).astype(v.dtype)


def setup_inputs(seed: int = 0) -> dict:
    key = jax.random.key(seed)
    ks = jax.random.split(key, 18)
    f32 = jnp.float32
    nrm = lambda k, shape, scale: (jax.random.normal(k, shape, f32) * scale)
    gain = lambda k, shape: 1.0 + 0.05 * jax.random.normal(k, shape, f32)
    L = DEPTH
    return {
        "x": jax.random.normal(ks[0], (BATCH, SEQ, D_MODEL), f32),
        "mem": jax.random.normal(ks[1], (BATCH, N_MEM, D_MODEL), f32),
        "g_mix": gain(ks[2], (L, D_MODEL)),
        "g_mem": gain(ks[3], (L, D_MODEL)),
        "w_in": nrm(ks[4], (L, D_MODEL, D_IN), D_MODEL ** -0.5),
        "conv_w": nrm(ks[5], (L, CONV_WIDTH, D_CONV), CONV_WIDTH ** -0.5),
        "pool_w": nrm(ks[6], (L, POOL_GROUPS, POOL_GROUP_DIM, POOL_GROUP_DIM), POOL_GROUP_DIM ** -0.5),
        "pool_scale": gain(ks[7], (L, D_POOL)),
        "w_kv": nrm(ks[8], (L, D_MODEL, 2 * D_XATT), D_MODEL ** -0.5),
        "w_out": nrm(ks[9], (L, D_MIX, D_MODEL), D_MIX ** -0.5),
        "g_ffn": gain(ks[10], (L, D_MODEL)),
        "w_up": nrm(ks[11], (L, D_MODEL, 2 * D_FF), D_MODEL ** -0.5),
        "ffn_conv_w": nrm(ks[12], (L, FFN_CONV_WIDTH, 2 * D_FF), FFN_CONV_WIDTH ** -0.5),
        "ffn_conv_b": nrm(ks[13], (L, 2 * D_FF), 0.01),
        "w_down": nrm(ks[14], (L, D_FF, D_MODEL), D_FF ** -0.5),
        "g_final": gain(ks[15], (D_MODEL,)),
    }


def reference(x, mem, g_mix, g_mem, w_in, conv_w, pool_w, pool_scale, w_kv, w_out,
              g_ffn, w_up, ffn_conv_w, ffn_conv_b, w_down, g_final):
    b, s, _ = x.shape
    split_pts = [D_CONV, 2 * D_CONV, 3 * D_CONV, 3 * D_CONV + D_POOL]
    att_scale = 1.0 / math.sqrt(XATT_HEAD_DIM)
    for l in range(DEPTH):
        h = rmsnorm(x, g_mix[l])
        z = h @ w_in[l]
        cb, cc, cx, pv, q = jnp.split(z, split_pts, axis=-1)

        conv_out = cb * causal_dwconv(cc * cx, conv_w[l])

        pooled = causal_pool_minus_self(pv.reshape(b, s, POOL_GROUPS, POOL_GROUP_DIM))
        pool_out = jnp.einsum('bsgc,gcd->bsgd', pooled, pool_w[l]).reshape(b, s, D_POOL)
        pool_out = pool_out * pool_scale[l]

        m = rmsnorm(mem, g_mem[l])
        kv = m @ w_kv[l]
        k, v = jnp.split(kv, 2, axis=-1)
        k = k.reshape(b, N_MEM, XATT_HEADS, XATT_HEAD_DIM)
        v = v.reshape(b, N_MEM, XATT_HEADS, XATT_HEAD_DIM)
        qh = q.reshape(b, s, XATT_HEADS, XATT_HEAD_DIM)
        scores = jnp.einsum('bshd,bmhd->bhsm', qh, k).astype(jnp.float32) * att_scale
        probs = jax.nn.softmax(scores, axis=-1).astype(x.dtype)
        att = jnp.einsum('bhsm,bmhd->bshd', probs, v).reshape(b, s, D_XATT)

        mix = jnp.concatenate([conv_out, pool_out, att], axis=-1)
        x = x + mix @ w_out[l]

        h = rmsnorm(x, g_ffn[l])
        u = causal_dwconv(h @ w_up[l], ffn_conv_w[l]) + ffn_conv_b[l]
        gate, val = jnp.split(u, 2, axis=-1)
        x = x + (jax.nn.silu(gate) * val) @ w_down[l]
    return rmsnorm(x, g_final)
```

```python
import numpy as np
from contextlib import ExitStack
import concourse.bass as bass
import concourse.mybir as mybir
from concourse.bass_utils import run_bass_kernel_spmd

F32 = mybir.dt.float32
BF16 = mybir.dt.bfloat16
AF = mybir.ActivationFunctionType
ALU = mybir.AluOpType

NCORES = 8
D = 2048
SEQ = 16384
TOK = SEQ // NCORES
HALO = 32
NJOB = 2
JNEW = TOK // NJOB
T = JNEW + HALO
XCOLS = TOK + HALO
NMEM = 256
DFF = 5632
NPAIR = DFF // 128
GK = 11
NGRP = NPAIR // GK
ARING = 13
EPS = 1e-6
ATT_SCALE = 1.0 / np.sqrt(128.0)
POOL_K = (2, 4, 8, 16)

MT = [(30, 374), (374, 715), (715, 1056)]
FT = [(32, 374), (374, 715), (715, 1056)]
NB1 = [(0, 374), (374, 715), (715, 1056)]

RING = 8
XLAG = 4
SLOT = 16 * 128

C_GMIX = 0
C_GMEM = 16
C_GFFN = 32
C_GFIN = 48
C_CONVW = 64
C_POOLSC = C_CONVW + 18
C_FFNW = C_POOLSC + 8
C_FFNB = C_FFNW + 264
C_CORR = C_FFNB + 88
C_HMASK = C_CORR + 64
C_EPS = C_HMASK + 1
NCST = C_EPS + 1


def _job_slabs():
    s = []
    for h in range(4):
        s.append(("q", h))
    for pc in (3, 4, 5, 0, 1, 2):
        s.append(("pv", pc))
    for i in range(6):
        s.append(("cx", i)); s.append(("cc", i)); s.append(("cb", i))
    for m in range(16):
        s.append(("wout", m))
    done = 0
    for g in range(NGRP):
        hi = min(NPAIR, GK * (g + 1) + 1)
        for j in range(done, hi):
            s.append(("upg", j)); s.append(("upv", j))
        done = hi
        for m in range(16):
            s.append(("down", g, m))
    return s


def _slab_size(s):
    k = s[0]
    if k in ("q", "cx", "cc", "cb", "upg", "upv", "kh", "vh", "pv", "wout"):
        return 16 * 128
    if k == "down":
        return GK * 128
    raise ValueError(k)


KV_SLABS = [("kh", h) for h in range(4)] + [("vh", h) for h in range(4)]
JOB_SLABS = _job_slabs()
PW_TILES = [(0, 0), (1, 0), (0, 1), (1, 1), (2, 1), (1, 2), (2, 2),
            (3, 3), (4, 3), (3, 4), (4, 4), (5, 4), (4, 5), (5, 5)]
POOLW_SIZE = len(PW_TILES) * 128
W_OFF = {}
_o = 0
for _s in KV_SLABS:
    W_OFF[_s] = _o; _o += _slab_size(_s)
W_OFF["poolw"] = _o; _o += POOLW_SIZE
for _s in JOB_SLABS:
    W_OFF[_s] = _o; _o += _slab_size(_s)
WTOT = _o


def _kslab(w, c0, nc_):
    K = w.shape[0]
    a = w[:, c0:c0 + nc_].reshape(K // 128, 128, nc_).transpose(1, 0, 2)
    return np.ascontiguousarray(a).reshape(128, -1)


def _build_wstream(w_in, pool_w, w_kv, w_out, w_up, w_down):
    ws = np.zeros((128, WTOT), np.float32)

    def put(key, arr):
        o = W_OFF[key]
        ws[:, o:o + arr.shape[1]] = arr

    for h in range(4):
        put(("kh", h), _kslab(w_kv, h * 128, 128))
        put(("vh", h), _kslab(w_kv, 512 + h * 128, 128))
    PW = np.zeros((768, 768), np.float32)
    for g in range(4):
        PW[192 * g:192 * g + 192, 192 * g:192 * g + 192] = pool_w[g]
    pw = np.zeros((128, len(PW_TILES), 128), np.float32)
    for i, (pc, oc) in enumerate(PW_TILES):
        pw[:, i, :] = PW[128 * pc:128 * pc + 128, 128 * oc:128 * oc + 128]
    put("poolw", pw.reshape(128, -1))
    for s in JOB_SLABS:
        k = s[0]
        if k == "q":
            put(s, _kslab(w_in, 3072 + s[1] * 128, 128))
        elif k == "pv":
            put(s, _kslab(w_in, 2304 + 128 * s[1], 128))
        elif k == "cb":
            put(s, _kslab(w_in, s[1] * 128, 128))
        elif k == "cc":
            put(s, _kslab(w_in, 768 + s[1] * 128, 128))
        elif k == "cx":
            put(s, _kslab(w_in, 1536 + s[1] * 128, 128))
        elif k == "wout":
            put(s, _kslab(w_out, s[1] * 128, 128))
        elif k == "upg":
            put(s, _kslab(w_up, s[1] * 128, 128))
        elif k == "upv":
            put(s, _kslab(w_up, DFF + s[1] * 128, 128))
        elif k == "down":
            g, m = s[1], s[2]
            a = w_down[g * GK * 128:(g + 1) * GK * 128, m * 128:(m + 1) * 128]
            a = a.reshape(GK, 128, 128).transpose(1, 0, 2)
            put(s, np.ascontiguousarray(a).reshape(128, -1))
    return ws


def _vec16(v):
    return np.ascontiguousarray(v.reshape(-1, 128).T)


def _build_consts(core, g_mix, g_mem, g_ffn, g_final, conv_w, pool_scale, ffn_conv_w, ffn_conv_b):
    c = np.zeros((128, NCST), np.float32)
    c[:, C_GMIX:C_GMIX + 16] = _vec16(g_mix)
    c[:, C_GMEM:C_GMEM + 16] = _vec16(g_mem)
    c[:, C_GFFN:C_GFFN + 16] = _vec16(g_ffn)
    c[:, C_GFIN:C_GFIN + 16] = _vec16(g_final)
    for i in range(6):
        for j in range(3):
            c[:, C_CONVW + i * 3 + j] = conv_w[j, i * 128:(i + 1) * 128]
    for oc in range(6):
        c[:, C_POOLSC + oc] = pool_scale[128 * oc:128 * oc + 128]
    for ch in range(88):
        for j in range(3):
            c[:, C_FFNW + ch * 3 + j] = ffn_conv_w[j, ch * 128:(ch + 1) * 128]
        c[:, C_FFNB + ch] = ffn_conv_b[ch * 128:(ch + 1) * 128]
    for w, k in enumerate(POOL_K):
        for i in range(16):
            c[:, C_CORR + w * 16 + i] = (float(k) / float(min(i + 1, k))) if core == 0 else 1.0
    c[:, C_HMASK] = 0.0 if core == 0 else 1.0
    c[:, C_EPS] = EPS
    return c


class Prog:
    def __init__(self):
        self.ops = []

    def op(self, eng, fn, reads=(), writes=(), chain=None):
        self.ops.append(dict(eng=eng, fn=fn, reads=tuple(reads), writes=tuple(writes), chain=chain))
        return len(self.ops) - 1


def _emit(nc, prog, stack, final_chains):
    ops = prog.ops
    DMA_ENGS = ("sp", "gq")
    last_w = {}
    readers = {}
    eng_seq = {}
    chain_seq = {}
    issue_know = {}
    prev_clock = {}

    def kmax(dst, src_):
        for k, v in src_.items():
            if dst.get(k, 0) < v:
                dst[k] = v

    for i, o in enumerate(ops):
        e = o["eng"]
        deps = set()
        raw = set()
        for r in o["reads"]:
            if r in last_w:
                deps.add(last_w[r]); raw.add(last_w[r])
        for w in o["writes"]:
            if w in last_w:
                deps.add(last_w[w])
            for rd in readers.get(w, ()):
                deps.add(rd)
        deps.discard(i)
        keep = []
        for d in deps:
            de = ops[d]["eng"]
            if de == e:
                if e == "pe":
                    continue
                keep.append(d)
                continue
            keep.append(d)
        for w in o["writes"]:
            last_w[w] = i
            readers[w] = []
        for r in o["reads"]:
            if r not in o["writes"]:
                readers.setdefault(r, []).append(i)
        if e in DMA_ENGS:
            ch = o["chain"]
            assert ch is not None
            n = chain_seq.get(ch, 0)
            o["sig"] = (("c", ch), 16 * (n + 1))
            chain_seq[ch] = n + 1
        else:
            n = eng_seq.get(e, 0)
            o["sig"] = (("e", e), n + 1)
            eng_seq[e] = n + 1
        know = issue_know.setdefault(e, {})
        waits = []
        for d in sorted(keep, reverse=True):
            key, val = ops[d]["sig"]
            if know.get(key, 0) >= val:
                continue
            waits.append((key, val, ops[d]["clock"].get(("e", "pe"), 0)))
            kmax(know, ops[d]["clock"])
        o["waits"] = waits
        clk = dict(know)
        if e not in DMA_ENGS and e in prev_clock:
            kmax(clk, prev_clock[e])
        key, val = o["sig"]
        if clk.get(key, 0) < val:
            clk[key] = val
        o["clock"] = clk
        if e not in DMA_ENGS:
            prev_clock[e] = clk
    sems = {}

    def sem(key):
        if key not in sems:
            nm = "s_" + "_".join(str(x) for x in (key[1] if isinstance(key[1], tuple) else (key[1],)))
            sems[key] = stack.enter_context(nc.semaphore(nm))
        return sems[key]

    for o in ops:
        sem(o["sig"][0])
    block = stack.enter_context(nc.Block())

    def run(engname, eng):
        prev_free = []
        prev_sig = 0
        for o in ops:
            if o["eng"] != engname:
                continue
            waits = list(o["waits"])
            attach = None
            standalone = []
            hoisted = []
            if engname == "pe":
                for w in waits:
                    safe = w[2] < prev_sig
                    if safe and prev_free:
                        hoisted.append((prev_free.pop(), w))
                    elif attach is None:
                        attach = w
                    else:
                        standalone.append(w)
            elif engname not in DMA_ENGS and waits:
                attach = waits[0]
                standalone = waits[1:]
            else:
                standalone = waits
            for ins_, w in hoisted:
                ins_._wait_ge(sem(w[0]), w[1])
            for w in standalone:
                eng.wait_ge(sem(w[0]), w[1])
            ret = o["fn"](eng)
            if isinstance(ret, list):
                allins = ret
            elif isinstance(ret, tuple):
                allins = [ret[0], ret[1]] if ret[0] is not ret[1] else [ret[0]]
            else:
                allins = [ret]
            first, last = allins[0], allins[-1]
            if attach is not None:
                first._wait_ge(sem(attach[0]), attach[1])
            key, val = o["sig"]
            last.then_inc(sem(key), 16 if key[0] == "c" else 1)
            if engname == "pe":
                prev_free = allins[1:] if attach is not None else allins[:]
                prev_sig = val
        if engname == "sp":
            for ch in final_chains:
                if ch in chain_seq:
                    eng.wait_ge(sem(("c", ch)), 16 * chain_seq[ch])

    @block.sync
    def _(e):
        run("sp", e)

    @block.gpsimd
    def _(e):
        run("gq", e)

    @block.tensor
    def _(e):
        run("pe", e)

    @block.scalar
    def _(e):
        run("act", e)

    @block.vector
    def _(e):
        run("dve", e)


def build_nc():
    nc = bass.Bass("TRN2", target_bir_lowering=False)
    xT = nc.dram_tensor("xT", [D, XCOLS], F32, kind="ExternalInput").ap()
    memT = nc.dram_tensor("memT", [D, NMEM], F32, kind="ExternalInput").ap()
    wst = nc.dram_tensor("wst", [128, WTOT], F32, kind="ExternalInput").ap()
    cstd = nc.dram_tensor("cst", [128, NCST], F32, kind="ExternalInput").ap()
    oT = nc.dram_tensor("oT", [D, TOK], F32, kind="ExternalOutput").ap()

    stack = ExitStack()
    sb = lambda name, shape, dt: stack.enter_context(nc.sbuf_tensor(name, shape, dt))
    xbuf = sb("xbuf", [128, 16, T], F32)
    hbuf = sb("hbuf", [128, 16, T], BF16)
    ubuf = sb("ubuf", [128, 16, T], BF16)
    wring = sb("wring", [128, RING, SLOT], BF16)
    cst = sb("cstsb", [128, NCST], F32)
    ones = sb("ones", [128, 128], BF16)
    KT = sb("KT", [128, 4, NMEM], BF16)
    Vb = sb("Vb", [128, 2, 512], BF16)
    pwb = sb("pwb", [128, POOLW_SIZE], BF16)
    NSF = 10
    NSB = 6
    SW = 376
    SWB = 344
    scf = sb("scf", [128, NSF, SW], F32)
    scb = sb("scb", [128, NSB, SW], BF16)
    qb = sb("qb", [128, 3, SWB], BF16)
    eb = sb("eb", [128, 4, SWB], BF16)
    pb = sb("pb", [128, 6, SWB], BF16)
    acc = sb("acc", [128, 3, SWB], F32)
    ones_f = sb("ones_f", [128, 128], F32)
    psum = [stack.enter_context(nc.psum_tensor("ps%d" % b, [128, 512], F32)) for b in range(8)]

    P = Prog()
    cnt = {"f": 0, "b": 0, "slab": 0, "q": 0, "e": 0, "p": 0, "proj": 0, "u": 0, "d": 0}

    def rot(name, n):
        i = cnt[name] % n
        cnt[name] += 1
        return i

    new_f = lambda: rot("f", NSF)
    new_b = lambda: rot("b", NSB)
    proj_bank = lambda: rot("proj", 4)

    def cc(col):
        return cst[:, col:col + 1]

    def pw(pc, oc):
        i = PW_TILES.index((pc, oc))
        return pwb[:, i * 128:(i + 1) * 128]

    XB = [[("x", c, bi) for c in range(16)] for bi in range(3)]

    slab_reads = {}

    def load_slab(key):
        n = cnt["slab"]; cnt["slab"] += 1
        slot = n % RING
        size = _slab_size(key)
        off = W_OFF[key]
        P.op("gq", lambda e, slot=slot, size=size, off=off: e.dma_start(out=wring[:, slot, 0:size], in_=wst[:, off:off + size]),
             reads=slab_reads.get(n, []), writes=[("wr", slot)], chain=("wr", slot))
        return slot

    def mm_group(bank, ncols, pairs, reads):
        def fn(e, bank=bank, ncols=ncols, pairs=pairs):
            n = len(pairs)
            return [e.matmul(psum[bank][0:mrows, 0:ncols], lhsT=l, rhs=r, start=(i == 0), stop=(i == n - 1))
                    for i, (l, r, mrows) in enumerate(pairs)]
        P.op("pe", fn, reads=reads, writes=[("ps", bank)])

    def hres(t):
        r = [("h", k, t) for k in range(16)]
        if t > 0:
            r += [("h", k, t - 1) for k in range(16)]
        return r

    HALL = [("h", k, t) for k in range(16) for t in range(3)]

    def proj(slot, rows, lo, hi, t):
        bank = proj_bank()
        mm_group(bank, hi - lo, [(wring[:, slot, k * rows:(k + 1) * rows], hbuf[:, k, lo:hi], rows) for k in range(16)],
                 reads=[("wr", slot)] + hres(t))
        return bank

    def sq_accum(c, blk, bank, bi):
        s, e_ = blk
        w = e_ - s
        sq = new_b()
        P.op("act", lambda e, c=c, s=s, e_=e_, w=w, sq=sq: e.activation(out=scb[:, sq, 0:w], in_=xbuf[:, c, s:e_], func=AF.Square),
             reads=[("x", c, bi)], writes=[("scb", sq)])

        def pe_part():
            P.op("pe", lambda e, c=c, w=w, sq=sq, bank=bank: e.matmul(psum[bank][:, 0:w], lhsT=ones[:, :], rhs=scb[:, sq, 0:w], start=(c == 0), stop=(c == 15)),
                 reads=[("scb", sq), ("ones",)], writes=[("ps", bank)])
        return pe_part

    def sq_acc_dve(c, blk, t):
        s, e_ = blk
        w = e_ - s
        sq = new_b()
        P.op("act", lambda e, c=c, s=s, e_=e_, w=w, sq=sq: e.activation(out=scb[:, sq, 0:w], in_=xbuf[:, c, s:e_], func=AF.Square),
             reads=[("x", c, t)], writes=[("scb", sq)])

        def dve_part():
            if c == 0:
                P.op("dve", lambda e, w=w, sq=sq, t=t: e.tensor_copy(out=acc[:, t, 0:w], in_=scb[:, sq, 0:w]),
                     reads=[("scb", sq)], writes=[("acc", t)])
            else:
                P.op("dve", lambda e, w=w, sq=sq, t=t: e.tensor_tensor(out=acc[:, t, 0:w], in0=acc[:, t, 0:w], in1=scb[:, sq, 0:w], op=ALU.add),
                     reads=[("scb", sq), ("acc", t)], writes=[("acc", t)])
        return dve_part

    def stats_finish(blk, t, bank):
        w = blk[1] - blk[0]
        P.op("pe", lambda e, w=w, t=t, bank=bank: e.matmul(psum[bank][:, 0:w], lhsT=ones_f[:, :], rhs=acc[:, t, 0:w], start=True, stop=True),
             reads=[("acc", t), ("onesf",)], writes=[("ps", bank)])

    def rstd_of(blk, bank):
        s, e_ = blk
        w = e_ - s
        f = new_f()
        P.op("act", lambda e, w=w, f=f, bank=bank: e.activation(out=scf[:, f, 0:w], in_=psum[bank][:, 0:w], func=AF.Ln, bias=cc(C_EPS), scale=1.0 / D),
             reads=[("ps", bank), ("cst",)], writes=[("scf", f)])
        r = new_f()
        P.op("act", lambda e, w=w, f=f, r=r: e.activation(out=scf[:, r, 0:w], in_=scf[:, f, 0:w], func=AF.Exp, scale=-0.5),
             reads=[("scf", f)], writes=[("scf", r)])
        return r

    def norm_apply(c, blk, bi, r, gcol, to_h):
        s, e_ = blk
        w = e_ - s
        if to_h:
            P.op("dve", lambda e, c=c, s=s, e_=e_, w=w, r=r: e.scalar_tensor_tensor(out=hbuf[:, c, s:e_], in0=xbuf[:, c, s:e_], scalar=cc(gcol + c), in1=scf[:, r, 0:w], op0=ALU.mult, op1=ALU.mult),
                 reads=[("x", c, bi), ("scf", r), ("cst",)], writes=[("h", c, bi)])
        else:
            P.op("dve", lambda e, c=c, s=s, e_=e_, w=w, r=r: e.scalar_tensor_tensor(out=xbuf[:, c, s:e_], in0=xbuf[:, c, s:e_], scalar=cc(gcol + c), in1=scf[:, r, 0:w], op0=ALU.mult, op1=ALU.mult),
                 reads=[("x", c, bi), ("scf", r), ("cst",)], writes=[("x", c, bi)])

    NBANK = (4, 5, 6)
    NBANK_NEXT = (3, 4, 5)

    P.op("sp", lambda e: e.dma_start(out=cst[:, :], in_=cstd[:, :]), writes=[("cst",)], chain=("cst",))
    P.op("dve", lambda e: e.memset(ones[:, :], 1.0), writes=[("ones",)])
    P.op("dve", lambda e: e.memset(ones_f[:, :], 1.0), writes=[("onesf",)])
    P.op("gq", lambda e: e.dma_start(out=pwb[:, :], in_=wst[:, W_OFF["poolw"]:W_OFF["poolw"] + POOLW_SIZE]),
         writes=[("pwb",)], chain=("pwb",))
    memv = memT.rearrange("(c p) m -> c p m", p=128)
    xv = xT.rearrange("(c p) t -> c p t", p=128)
    ov = oT.rearrange("(c p) t -> c p t", p=128)
    mT = ubuf[:, 12:16, :].rearrange("p a b -> p (a b)")

    xv4 = xT.rearrange("(c p) t -> p c t", p=128)
    ov4 = oT.rearrange("(c p) t -> p c t", p=128)

    def load_x(job_, bi, cg, extra=()):
        c0_ = job_ * JNEW
        s, e_ = NB1[bi]
        P.op("sp", lambda e, cg=cg, c0_=c0_, s=s, e_=e_: e.dma_start(out=xbuf[:, 4 * cg:4 * cg + 4, s:e_], in_=xv4[:, 4 * cg:4 * cg + 4, c0_ + s:c0_ + e_]),
             reads=list(extra), writes=[("x", c, bi) for c in range(4 * cg, 4 * cg + 4)], chain=("x", cg, bi))

    def store_o(job_, t, cg):
        c0_ = job_ * JNEW
        a0, b0 = FT[t]
        P.op("sp", lambda e, cg=cg, c0_=c0_, a0=a0, b0=b0: e.dma_start(out=ov4[:, 4 * cg:4 * cg + 4, c0_ + a0 - HALO:c0_ + b0 - HALO], in_=xbuf[:, 4 * cg:4 * cg + 4, a0:b0]),
             reads=[("x", c, t) for c in range(4 * cg, 4 * cg + 4)], chain=("o", cg, t))

    for cg in range(4):
        load_x(0, 0, cg)
    slab_reads[0] = XB[0]
    slab_reads[6] = XB[2]

    def kv_prologue():
        MBANK = 7
        for c in range(16):
            f = new_f()
            P.op("sp", lambda e, c=c, f=f: e.dma_start(out=scf[:, f, 0:NMEM], in_=memv[c]), writes=[("scf", f)], chain=("scfd", f))
            sq = new_b()
            P.op("act", lambda e, f=f, sq=sq: e.activation(out=scb[:, sq, 0:NMEM], in_=scf[:, f, 0:NMEM], func=AF.Square),
                 reads=[("scf", f)], writes=[("scb", sq)])
            P.op("pe", lambda e, c=c, sq=sq: e.matmul(psum[MBANK][:, 0:NMEM], lhsT=ones[:, :], rhs=scb[:, sq, 0:NMEM], start=(c == 0), stop=(c == 15)),
                 reads=[("scb", sq), ("ones",)], writes=[("ps", MBANK)])
        rm = rstd_of((0, NMEM), MBANK)
        rmk = acc[:, 0, 0:NMEM]
        P.op("act", lambda e: e.activation(out=rmk, in_=scf[:, rm, 0:NMEM], func=AF.Copy), reads=[("scf", rm)], writes=[("acc", 0)])
        for c in range(16):
            f = new_f()
            P.op("sp", lambda e, c=c, f=f: e.dma_start(out=scf[:, f, 0:NMEM], in_=memv[c]), writes=[("scf", f)], chain=("scfd", f))
            P.op("dve", lambda e, c=c, f=f: e.scalar_tensor_tensor(out=mT[:, c * NMEM:(c + 1) * NMEM], in0=scf[:, f, 0:NMEM], scalar=cc(C_GMEM + c), in1=rmk, op0=ALU.mult, op1=ALU.mult),
                 reads=[("scf", f), ("acc", 0), ("cst",)], writes=[("u", 12), ("u", 13), ("u", 14), ("u", 15)])
        MTR = [("u", 12), ("u", 13), ("u", 14), ("u", 15)]
        for h in range(4):
            slot = load_slab(("kh", h))
            bank = proj_bank()
            mm_group(bank, NMEM, [(wring[:, slot, k * 128:(k + 1) * 128], mT[:, k * NMEM:(k + 1) * NMEM], 128) for k in range(16)],
                     reads=[("wr", slot)] + MTR)
            P.op("act", lambda e, h=h, bank=bank: e.activation(out=KT[:, h, :], in_=psum[bank][:, 0:NMEM], func=AF.Copy),
                 reads=[("ps", bank)], writes=[("KT",)])
        for h in range(4):
            slot = load_slab(("vh", h))
            for mc in range(2):
                bank = proj_bank()
                mm_group(bank, 128, [(mT[:, k * NMEM + mc * 128:k * NMEM + (mc + 1) * 128], wring[:, slot, k * 128:(k + 1) * 128], 128) for k in range(16)],
                         reads=[("wr", slot)] + MTR)
                P.op("act", lambda e, h=h, mc=mc, bank=bank: e.activation(out=Vb[:, mc, h * 128:(h + 1) * 128], in_=psum[bank][:, 0:128], func=AF.Copy),
                     reads=[("ps", bank)], writes=[("Vb",)])

    def stats_block(bi):
        for c in range(16):
            sq_accum(c, NB1[bi], NBANK[bi], bi)()

    def norm1_block(bi):
        r_ = rstd_of(NB1[bi], NBANK[bi])
        for c in range(16):
            norm_apply(c, NB1[bi], bi, r_, C_GMIX, True)

    for job in range(NJOB):
        c0 = job * JNEW

        def att_head(h):
            slot = load_slab(("q", h))
            qs = {}
            es = {}

            def U(t):
                m0, b0 = MT[t]
                w = b0 - m0
                bank = proj(slot, 128, m0, b0, t)
                q = rot("q", 3)
                P.op("act", lambda e, w=w, q=q, bank=bank: e.activation(out=qb[:, q, 0:w], in_=psum[bank][:, 0:w], func=AF.Copy),
                     reads=[("ps", bank)], writes=[("qb", q)])
                qs[t] = q

            def S(t):
                m0, b0 = MT[t]
                w = b0 - m0
                et = []
                for mc in range(2):
                    bank = 4 + 2 * (t % 2) + mc
                    mm_group(bank, w, [(KT[:, h, mc * 128:(mc + 1) * 128], qb[:, qs[t], 0:w], 128)],
                             reads=[("KT",), ("qb", qs[t])])
                    ei = rot("e", 4)
                    P.op("act", lambda e, w=w, ei=ei, bank=bank: e.activation(out=eb[:, ei, 0:w], in_=psum[bank][:, 0:w], func=AF.Exp, scale=float(ATT_SCALE)),
                         reads=[("ps", bank)], writes=[("eb", ei)])
                    et.append(ei)
                es[t] = et

            def A(t):
                m0, b0 = MT[t]
                w = b0 - m0
                ba = proj_bank()
                bs = proj_bank()
                mm_group(ba, w, [(Vb[:, mc, h * 128:(h + 1) * 128], eb[:, es[t][mc], 0:w], 128) for mc in range(2)],
                         reads=[("Vb",)] + [("eb", es[t][mc]) for mc in range(2)])
                mm_group(bs, w, [(ones[:, :], eb[:, es[t][mc], 0:w], 128) for mc in range(2)],
                         reads=[("ones",)] + [("eb", es[t][mc]) for mc in range(2)])
                fl = new_f()
                P.op("act", lambda e, w=w, fl=fl, bs=bs: e.activation(out=scf[:, fl, 0:w], in_=psum[bs][:, 0:w], func=AF.Ln),
                     reads=[("ps", bs)], writes=[("scf", fl)])
                f = new_f()
                P.op("act", lambda e, w=w, fl=fl, f=f: e.activation(out=scf[:, f, 0:w], in_=scf[:, fl, 0:w], func=AF.Exp, scale=-1.0),
                     reads=[("scf", fl)], writes=[("scf", f)])
                P.op("dve", lambda e, w=w, f=f, ba=ba, m0=m0, b0=b0: e.tensor_tensor(out=ubuf[:, 12 + h, m0:b0], in0=psum[ba][:, 0:w], in1=scf[:, f, 0:w], op=ALU.mult),
                     reads=[("ps", ba), ("scf", f)], writes=[("u", 12 + h)])

            U(0); U(1); S(0); U(2); S(1); A(0); S(2); A(1); A(2)

        lin_q = []

        PC_WIN = {0: (2, 2), 1: (2, 4), 2: (4, 4), 3: (8, 8), 4: (8, 16), 5: (16, 16)}

        def pool_step(pc, t, slot):
            klo, khi = PC_WIN[pc]
            m0, b0 = MT[t]
            w = b0 - m0
            wn = w + 15
            bank = proj(slot, 128, m0 - 15, b0, t)
            v = new_f()
            P.op("act", lambda e, wn=wn, v=v, bank=bank: e.activation(out=scf[:, v, 0:wn], in_=psum[bank][:, 0:wn], func=AF.Copy),
                 reads=[("ps", bank)], writes=[("scf", v)])
            sums = {1: v}
            cur = v
            sh = 1
            while sh < khi:
                nx = new_f()
                P.op("dve", lambda e, wn=wn, cur=cur, nx=nx, sh=sh: e.tensor_tensor(out=scf[:, nx, sh:wn], in0=scf[:, cur, sh:wn], in1=scf[:, cur, 0:wn - sh], op=ALU.add),
                     reads=[("scf", cur)], writes=[("scf", nx)])
                cur = nx
                sh *= 2
                sums[sh] = cur
            pi = rot("p", 6)
            halves = [(0, 128, klo)] if klo == khi else [(0, 64, klo), (64, 128, khi)]
            for (r0, r1, kk) in halves:
                sk = sums[kk]
                wi = POOL_K.index(kk)
                if job == 0 and t == 0:
                    P.op("dve", lambda e, r0=r0, r1=r1, sk=sk, wi=wi: e.tensor_tensor(out=scf[r0:r1, sk, 17:33], in0=scf[r0:r1, sk, 17:33], in1=cst[r0:r1, C_CORR + 16 * wi:C_CORR + 16 * wi + 16], op=ALU.mult),
                         reads=[("scf", sk), ("cst",)], writes=[("scf", sk)])
                P.op("dve", lambda e, r0=r0, r1=r1, w=w, sk=sk, v=v, pi=pi, kk=kk: e.scalar_tensor_tensor(out=pb[r0:r1, pi, 0:w], in0=scf[r0:r1, sk, 15:15 + w], scalar=1.0 / kk, in1=scf[r0:r1, v, 15:15 + w], op0=ALU.mult, op1=ALU.subtract),
                     reads=[("scf", sk), ("scf", v)], writes=[("pb", pi)])
            return pi

        def pool_linear(blk, t, pooled):
            m0, b0 = MT[t]
            w = b0 - m0
            for oc in range(3 * blk, 3 * blk + 3):
                ks = [pc for (pc, o_) in PW_TILES if o_ == oc]
                bank = proj_bank()
                mm_group(bank, w, [(pw(pc, oc), pb[:, pooled[pc - 3 * blk], 0:w], 128) for pc in ks],
                         reads=[("pwb",)] + [("pb", pooled[pc - 3 * blk]) for pc in ks])
                P.op("act", lambda e, bank=bank, oc=oc, m0=m0, b0=b0, w=w: e.activation(out=ubuf[:, 6 + oc, m0:b0], in_=psum[bank][:, 0:w], func=AF.Identity, scale=cc(C_POOLSC + oc)),
                     reads=[("ps", bank), ("cst",)], writes=[("u", 6 + oc)])

        def conv_tile(i, t, scx, scc, scbk):
            m0, b0 = MT[t]
            w = b0 - m0
            wn = w + 2
            bx = proj(scx, 128, m0 - 2, b0, t)
            fx = new_f()
            P.op("act", lambda e, wn=wn, fx=fx, bx=bx: e.activation(out=scf[:, fx, 0:wn], in_=psum[bx][:, 0:wn], func=AF.Copy),
                 reads=[("ps", bx)], writes=[("scf", fx)])
            bc = proj(scc, 128, m0 - 2, b0, t)
            fv = new_f()
            P.op("dve", lambda e, wn=wn, fx=fx, fv=fv, bc=bc: e.tensor_tensor(out=scf[:, fv, 0:wn], in0=psum[bc][:, 0:wn], in1=scf[:, fx, 0:wn], op=ALU.mult),
                 reads=[("ps", bc), ("scf", fx)], writes=[("scf", fv)])
            fc = new_f()
            P.op("act", lambda e, w=w, fv=fv, fc=fc, i=i: e.activation(out=scf[:, fc, 0:w], in_=scf[:, fv, 2:2 + w], func=AF.Identity, scale=cc(C_CONVW + 3 * i + 2)),
                 reads=[("scf", fv), ("cst",)], writes=[("scf", fc)])
            bb = proj(scbk, 128, m0, b0, t)
            P.op("dve", lambda e, w=w, fv=fv, fc=fc, i=i: e.scalar_tensor_tensor(out=scf[:, fc, 0:w], in0=scf[:, fv, 1:1 + w], scalar=cc(C_CONVW + 3 * i + 1), in1=scf[:, fc, 0:w], op0=ALU.mult, op1=ALU.add),
                 reads=[("scf", fv), ("scf", fc), ("cst",)], writes=[("scf", fc)])
            fc2 = new_f()
            P.op("dve", lambda e, w=w, fv=fv, fc=fc, fc2=fc2, i=i: e.scalar_tensor_tensor(out=scf[:, fc2, 0:w], in0=scf[:, fv, 0:w], scalar=cc(C_CONVW + 3 * i + 0), in1=scf[:, fc, 0:w], op0=ALU.mult, op1=ALU.add),
                 reads=[("scf", fv), ("scf", fc), ("cst",)], writes=[("scf", fc2)])
            P.op("dve", lambda e, w=w, fc2=fc2, bb=bb, i=i, m0=m0, b0=b0: e.tensor_tensor(out=ubuf[:, i, m0:b0], in0=psum[bb][:, 0:w], in1=scf[:, fc2, 0:w], op=ALU.mult),
                 reads=[("ps", bb), ("scf", fc2)], writes=[("u", i)])

        def conv_head(i):
            scx = load_slab(("cx", i))
            scc = load_slab(("cc", i))
            scbk = load_slab(("cb", i))
            for t in range(3):
                conv_tile(i, t, scx, scc, scbk)
                if lin_q:
                    lin_q.pop(0)()

        if job == 0:
            stats_block(0)
            norm1_block(0)
            sl0 = [load_slab(("cx", 0)), load_slab(("cc", 0)), load_slab(("cb", 0))]
            sl1 = [load_slab(("cx", 1)), load_slab(("cc", 1)), load_slab(("cb", 1))]
            for bi in (1, 2):
                for cg in range(4):
                    load_x(0, bi, cg, extra=[("wr", sl1[2])])
            stats_block(1)
            stats_block(2)
        else:
            sl0 = [load_slab(("cx", 0)), load_slab(("cc", 0)), load_slab(("cb", 0))]
            sl1 = [load_slab(("cx", 1)), load_slab(("cc", 1)), load_slab(("cb", 1))]
        conv_tile(0, 0, *sl0)
        norm1_block(1)
        conv_tile(1, 0, *sl1)
        conv_tile(0, 1, *sl0)
        norm1_block(2)
        conv_tile(1, 1, *sl1)
        conv_tile(0, 2, *sl0)
        conv_tile(1, 2, *sl1)
        conv_head(2)
        for blk in (1, 0):
            pslots = [load_slab(("pv", 3 * blk + i)) for i in range(3)]
            for t in range(3):
                pooled = [pool_step(3 * blk + i, t, pslots[i]) for i in range(3)]
                if lin_q:
                    lin_q.pop(0)()
                lin_q.append(lambda blk=blk, t=t, pooled=pooled: pool_linear(blk, t, pooled))
        if job == 0:
            kv_prologue()
        for h in range(4):
            att_head(h)
            if lin_q:
                lin_q.pop(0)()
        for i in (3, 4, 5):
            conv_head(i)
        while lin_q:
            lin_q.pop(0)()

        def wout_group(m, t, s0):
            m0, b0 = MT[t]
            w = b0 - m0
            bank = proj_bank()
            prs = [(wring[:, s0, kt * 128:(kt + 1) * 128], ubuf[:, kt, m0:b0], 128) for kt in range(16)]
            mm_group(bank, w, prs, reads=[("wr", s0)] + [("u", kt) for kt in range(16)])
            P.op("dve", lambda e, w=w, m=m, bank=bank, m0=m0, b0=b0: e.tensor_tensor(out=xbuf[:, m, m0:b0], in0=psum[bank][:, 0:w], in1=xbuf[:, m, m0:b0], op=ALU.add),
                 reads=[("ps", bank), ("x", m, t)], writes=[("x", m, t)])
            return sq_acc_dve(m, MT[t], t)

        def norm2_block(bi):
            stats_finish(MT[bi], bi, NBANK[bi])
            r_ = rstd_of(MT[bi], NBANK[bi])
            for c in range(16):
                norm_apply(c, MT[bi], bi, r_, C_GFFN, True)
            if job == 0 and bi == 0:
                P.op("dve", lambda e: e.tensor_scalar(out=hbuf[:, :, 30:32], in0=hbuf[:, :, 30:32], scalar1=cc(C_HMASK), scalar2=None, op0=ALU.mult),
                     reads=[("h", k, 0) for k in range(16)] + [("cst",)], writes=[("h", k, 0) for k in range(16)])

        pend = []
        for m in range(14):
            s0 = load_slab(("wout", m))
            newp = [wout_group(m, t, s0) for t in range(3)]
            for p_ in pend:
                p_()
            pend = newp
        sA = load_slab(("wout", 14))
        sB = load_slab(("wout", 15))
        for t in range(3):
            for (m, s_) in ((14, sA), (15, sB)):
                newp = [wout_group(m, t, s_)]
                for p_ in pend:
                    p_()
                pend = newp
            for p_ in pend:
                p_()
            pend = []
            norm2_block(t)

        def up_pair(j):
            sg_ = load_slab(("upg", j))
            sv_ = load_slab(("upv", j))
            for t in range(3):
                up_unit(j, t, sg_, sv_)

        def up_unit(j, t, sg_, sv_):
            aslot = j % ARING
            a0, b0 = FT[t]
            if True:
                w = b0 - a0
                wn = w + 2
                u = rot("u", 3)
                bg, bv = 2 * u, 2 * u + 1
                def fn(e, bg=bg, bv=bv, wn=wn, sg_=sg_, sv_=sv_, a0=a0, b0=b0):
                    out_ = []
                    for bank_, sl_ in ((bg, sg_), (bv, sv_)):
                        for k in range(16):
                            out_.append(e.matmul(psum[bank_][:, 0:wn], lhsT=wring[:, sl_, k * 128:(k + 1) * 128], rhs=hbuf[:, k, a0 - 2:b0], start=(k == 0), stop=(k == 15)))
                    return out_
                P.op("pe", fn, reads=[("wr", sg_), ("wr", sv_)] + hres(t), writes=[("ps", bg), ("ps", bv)])
                tg = new_f(); tv = new_f(); sgb = new_f()
                wg = C_FFNW + 3 * j
                wv = C_FFNW + 3 * (NPAIR + j)
                P.op("act", lambda e, w=w, tg=tg, bg=bg, wg=wg, j=j: e.activation(out=scf[:, tg, 0:w], in_=psum[bg][:, 2:2 + w], func=AF.Identity, bias=cc(C_FFNB + j), scale=cc(wg + 2)),
                     reads=[("ps", bg), ("cst",)], writes=[("scf", tg)])
                P.op("act", lambda e, w=w, tv=tv, bv=bv, wv=wv, j=j: e.activation(out=scf[:, tv, 0:w], in_=psum[bv][:, 2:2 + w], func=AF.Identity, bias=cc(C_FFNB + NPAIR + j), scale=cc(wv + 2)),
                     reads=[("ps", bv), ("cst",)], writes=[("scf", tv)])
                P.op("dve", lambda e, w=w, tg=tg, bg=bg, wg=wg: e.scalar_tensor_tensor(out=scf[:, tg, 0:w], in0=psum[bg][:, 1:1 + w], scalar=cc(wg + 1), in1=scf[:, tg, 0:w], op0=ALU.mult, op1=ALU.add),
                     reads=[("ps", bg), ("scf", tg), ("cst",)], writes=[("scf", tg)])
                P.op("dve", lambda e, w=w, tv=tv, bv=bv, wv=wv: e.scalar_tensor_tensor(out=scf[:, tv, 0:w], in0=psum[bv][:, 1:1 + w], scalar=cc(wv + 1), in1=scf[:, tv, 0:w], op0=ALU.mult, op1=ALU.add),
                     reads=[("ps", bv), ("scf", tv), ("cst",)], writes=[("scf", tv)])
                P.op("dve", lambda e, w=w, tg=tg, bg=bg, wg=wg: e.scalar_tensor_tensor(out=scf[:, tg, 0:w], in0=psum[bg][:, 0:w], scalar=cc(wg), in1=scf[:, tg, 0:w], op0=ALU.mult, op1=ALU.add),
                     reads=[("ps", bg), ("scf", tg), ("cst",)], writes=[("scf", tg)])
                P.op("dve", lambda e, w=w, tv=tv, bv=bv, wv=wv: e.scalar_tensor_tensor(out=scf[:, tv, 0:w], in0=psum[bv][:, 0:w], scalar=cc(wv), in1=scf[:, tv, 0:w], op0=ALU.mult, op1=ALU.add),
                     reads=[("ps", bv), ("scf", tv), ("cst",)], writes=[("scf", tv)])
                P.op("act", lambda e, w=w, tg=tg, sgb=sgb: e.activation(out=scf[:, sgb, 0:w], in_=scf[:, tg, 0:w], func=AF.Silu),
                     reads=[("scf", tg)], writes=[("scf", sgb)])
                P.op("dve", lambda e, w=w, tv=tv, sgb=sgb, aslot=aslot, a0=a0, b0=b0: e.tensor_tensor(out=ubuf[:, aslot, a0:b0], in0=scf[:, sgb, 0:w], in1=scf[:, tv, 0:w], op=ALU.mult),
                     reads=[("scf", sgb), ("scf", tv)], writes=[("u", aslot)])

        FBANK = (0, 1, 2)

        def down_group(g):
            pend = []
            for m in range(16):
                slot = load_slab(("down", g, m))
                newp = []
                for t, (a0, b0) in enumerate(FT):
                    w = b0 - a0
                    bank = 6 + rot("d", 2)
                    prs = [(wring[:, slot, kk * 128:(kk + 1) * 128], ubuf[:, (g * GK + kk) % ARING, a0:b0], 128) for kk in range(GK)]
                    mm_group(bank, w, prs, reads=[("wr", slot)] + [("u", (g * GK + kk) % ARING) for kk in range(GK)])
                    P.op("dve", lambda e, w=w, m=m, bank=bank, a0=a0, b0=b0: e.tensor_tensor(out=xbuf[:, m, a0:b0], in0=psum[bank][:, 0:w], in1=xbuf[:, m, a0:b0], op=ALU.add),
                         reads=[("ps", bank), ("x", m, t)], writes=[("x", m, t)])
                    if g == NGRP - 1:
                        newp.append(sq_acc_dve(m, FT[t], t))
                for p_ in pend:
                    p_()
                pend = newp
            for p_ in pend:
                p_()

        ps0 = (load_slab(("upg", 0)), load_slab(("upv", 0)))
        ps1 = (load_slab(("upg", 1)), load_slab(("upv", 1)))
        for t in range(3):
            up_unit(0, t, *ps0)
            up_unit(1, t, *ps1)
        done = 2
        for g in range(NGRP):
            hi = min(NPAIR, GK * (g + 1) + 1)
            for j in range(done, hi):
                up_pair(j)
            done = hi
            down_group(g)

        nxt = job + 1 < NJOB
        for bi, blk in enumerate(FT):
            stats_finish(blk, bi, FBANK[bi])
        rs_ = [rstd_of(blk, FBANK[bi]) for bi, blk in enumerate(FT)]
        for t in range(3):
            if t == 2 and nxt:
                stats_block(0)
                norm1_block(0)
            for c in range(16):
                norm_apply(c, FT[t], t, rs_[t], C_GFIN, False)
                if c % 4 == 3:
                    cg = c // 4
                    store_o(job, t, cg)
                    if nxt and cg >= 1:
                        load_x(job + 1, t, cg - 1)
            if nxt:
                load_x(job + 1, t, 3)
        if nxt:
            stats_block(1)
            stats_block(2)

    _emit(nc, P, stack, [("o", cg, t) for cg in range(4) for t in range(3)])
    stack.close()
    return nc


_NC_CACHE = {}


def kernel(x, mem, g_mix, g_mem, w_in, conv_w, pool_w, pool_scale, w_kv, w_out,
           g_ffn, w_up, ffn_conv_w, ffn_conv_b, w_down, g_final):
    f = lambda a: np.asarray(a, dtype=np.float32)
    x = f(x)[0]; mem = f(mem)[0]
    g_mix = f(g_mix)[0]; g_mem = f(g_mem)[0]; g_ffn = f(g_ffn)[0]; g_final = f(g_final)
    w_in = f(w_in)[0]; conv_w = f(conv_w)[0]; pool_w = f(pool_w)[0]; pool_scale = f(pool_scale)[0]
    w_kv = f(w_kv)[0]; w_out = f(w_out)[0]; w_up = f(w_up)[0]
    ffn_conv_w = f(ffn_conv_w)[0]; ffn_conv_b = f(ffn_conv_b)[0]; w_down = f(w_down)[0]

    ws = _build_wstream(w_in, pool_w, w_kv, w_out, w_up, w_down)
    memT = np.ascontiguousarray(mem.T)
    in_maps = []
    for c in range(NCORES):
        xt = np.zeros((D, XCOLS), np.float32)
        lo = c * TOK - HALO
        if lo < 0:
            xt[:, HALO:] = x[0:TOK].T
        else:
            xt[:, :] = x[lo:lo + XCOLS].T
        cst = _build_consts(c, g_mix, g_mem, g_ffn, g_final, conv_w, pool_scale, ffn_conv_w, ffn_conv_b)
        in_maps.append({"xT": xt, "memT": memT, "wst": ws, "cst": cst})
    if "nc" not in _NC_CACHE:
        _NC_CACHE["nc"] = build_nc()
    nc = _NC_CACHE["nc"]
    res = run_bass_kernel_spmd(nc, in_maps, core_ids=list(range(NCORES)))
    out = np.empty((1, SEQ, D), np.float32)
    for c in range(NCORES):
        out[0, c * TOK:(c + 1) * TOK, :] = res.results[c]["oT"].T
    return out
```

```python
import numpy as np
from contextlib import ExitStack
import concourse.bass as bass
import concourse.mybir as mybir
from concourse.bass_utils import run_bass_kernel_spmd

F32 = mybir.dt.float32
BF16 = mybir.dt.bfloat16
AF = mybir.ActivationFunctionType
ALU = mybir.AluOpType

NCORES = 8
D = 2048
SEQ = 16384
TOK = SEQ // NCORES
HALO = 32
NJOB = 2
JNEW = TOK // NJOB
T = JNEW + HALO
XCOLS = TOK + HALO
NMEM = 256
DFF = 5632
NPAIR = DFF // 128
GK = 11
NGRP = NPAIR // GK
ARING = 13
EPS = 1e-6
ATT_SCALE = 1.0 / np.sqrt(128.0)
POOL_K = (2, 4, 8, 16)

MT = [(30, 374), (374, 715), (715, 1056)]
FT = [(32, 374), (374, 715), (715, 1056)]
NB1 = [(0, 374), (374, 715), (715, 1056)]

RING = 8
XLAG = 4
SLOT = 16 * 128

C_GMIX = 0
C_GMEM = 16
C_GFFN = 32
C_GFIN = 48
C_CONVW = 64
C_POOLSC = C_CONVW + 18
C_FFNW = C_POOLSC + 8
C_FFNB = C_FFNW + 264
C_CORR = C_FFNB + 88
C_HMASK = C_CORR + 64
C_EPS = C_HMASK + 1
NCST = C_EPS + 1


def _job_slabs():
    s = []
    for h in range(4):
        s.append(("q", h))
    for pc in (3, 4, 5, 0, 1, 2):
        s.append(("pv", pc))
    for i in range(6):
        s.append(("cx", i)); s.append(("cc", i)); s.append(("cb", i))
    for m in range(16):
        s.append(("wout", m))
    done = 0
    for g in range(NGRP):
        hi = min(NPAIR, GK * (g + 1) + 1)
        for j in range(done, hi):
            s.append(("upg", j)); s.append(("upv", j))
        done = hi
        for m in range(16):
            s.append(("down", g, m))
    return s


def _slab_size(s):
    k = s[0]
    if k in ("q", "cx", "cc", "cb", "upg", "upv", "kh", "vh", "pv", "wout"):
        return 16 * 128
    if k == "down":
        return GK * 128
    raise ValueError(k)


KV_SLABS = [("kh", h) for h in range(4)] + [("vh", h) for h in range(4)]
JOB_SLABS = _job_slabs()
PW_TILES = [(0, 0), (1, 0), (0, 1), (1, 1), (2, 1), (1, 2), (2, 2),
            (3, 3), (4, 3), (3, 4), (4, 4), (5, 4), (4, 5), (5, 5)]
POOLW_SIZE = len(PW_TILES) * 128
W_OFF = {}
_o = 0
for _s in KV_SLABS:
    W_OFF[_s] = _o; _o += _slab_size(_s)
W_OFF["poolw"] = _o; _o += POOLW_SIZE
for _s in JOB_SLABS:
    W_OFF[_s] = _o; _o += _slab_size(_s)
WTOT = _o


def _kslab(w, c0, nc_):
    K = w.shape[0]
    a = w[:, c0:c0 + nc_].reshape(K // 128, 128, nc_).transpose(1, 0, 2)
    return np.ascontiguousarray(a).reshape(128, -1)


def _build_wstream(w_in, pool_w, w_kv, w_out, w_up, w_down):
    ws = np.zeros((128, WTOT), np.float32)

    def put(key, arr):
        o = W_OFF[key]
        ws[:, o:o + arr.shape[1]] = arr

    for h in range(4):
        put(("kh", h), _kslab(w_kv, h * 128, 128))
        put(("vh", h), _kslab(w_kv, 512 + h * 128, 128))
    PW = np.zeros((768, 768), np.float32)
    for g in range(4):
        PW[192 * g:192 * g + 192, 192 * g:192 * g + 192] = pool_w[g]
    pw = np.zeros((128, len(PW_TILES), 128), np.float32)
    for i, (pc, oc) in enumerate(PW_TILES):
        pw[:, i, :] = PW[128 * pc:128 * pc + 128, 128 * oc:128 * oc + 128]
    put("poolw", pw.reshape(128, -1))
    for s in JOB_SLABS:
        k = s[0]
        if k == "q":
            put(s, _kslab(w_in, 3072 + s[1] * 128, 128))
        elif k == "pv":
            put(s, _kslab(w_in, 2304 + 128 * s[1], 128))
        elif k == "cb":
            put(s, _kslab(w_in, s[1] * 128, 128))
        elif k == "cc":
            put(s, _kslab(w_in, 768 + s[1] * 128, 128))
        elif k == "cx":
            put(s, _kslab(w_in, 1536 + s[1] * 128, 128))
        elif k == "wout":
            put(s, _kslab(w_out, s[1] * 128, 128))
        elif k == "upg":
            put(s, _kslab(w_up, s[1] * 128, 128))
        elif k == "upv":
            put(s, _kslab(w_up, DFF + s[1] * 128, 128))
        elif k == "down":
            g, m = s[1], s[2]
            a = w_down[g * GK * 128:(g + 1) * GK * 128, m * 128:(m + 1) * 128]
            a = a.reshape(GK, 128, 128).transpose(1, 0, 2)
            put(s, np.ascontiguousarray(a).reshape(128, -1))
    return ws


def _vec16(v):
    return np.ascontiguousarray(v.reshape(-1, 128).T)


def _build_consts(core, g_mix, g_mem, g_ffn, g_final, conv_w, pool_scale, ffn_conv_w, ffn_conv_b):
    c = np.zeros((128, NCST), np.float32)
    c[:, C_GMIX:C_GMIX + 16] = _vec16(g_mix)
    c[:, C_GMEM:C_GMEM + 16] = _vec16(g_mem)
    c[:, C_GFFN:C_GFFN + 16] = _vec16(g_ffn)
    c[:, C_GFIN:C_GFIN + 16] = _vec16(g_final)
    for i in range(6):
        for j in range(3):
            c[:, C_CONVW + i * 3 + j] = conv_w[j, i * 128:(i + 1) * 128]
    for oc in range(6):
        c[:, C_POOLSC + oc] = pool_scale[128 * oc:128 * oc + 128]
    for ch in range(88):
        for j in range(3):
            c[:, C_FFNW + ch * 3 + j] = ffn_conv_w[j, ch * 128:(ch + 1) * 128]
        c[:, C_FFNB + ch] = ffn_conv_b[ch * 128:(ch + 1) * 128]
    for w, k in enumerate(POOL_K):
        for i in range(16):
            c[:, C_CORR + w * 16 + i] = (float(k) / float(min(i + 1, k))) if core == 0 else 1.0
    c[:, C_HMASK] = 0.0 if core == 0 else 1.0
    c[:, C_EPS] = EPS
    return c


class Prog:
    def __init__(self):
        self.ops = []

    def op(self, eng, fn, reads=(), writes=(), chain=None):
        self.ops.append(dict(eng=eng, fn=fn, reads=tuple(reads), writes=tuple(writes), chain=chain))
        return len(self.ops) - 1


def _emit(nc, prog, stack, final_chains):
    ops = prog.ops
    DMA_ENGS = ("sp", "gq")
    last_w = {}
    readers = {}
    eng_seq = {}
    chain_seq = {}
    issue_know = {}
    prev_clock = {}

    def kmax(dst, src_):
        for k, v in src_.items():
            if dst.get(k, 0) < v:
                dst[k] = v

    for i, o in enumerate(ops):
        e = o["eng"]
        deps = set()
        raw = set()
        for r in o["reads"]:
            if r in last_w:
                deps.add(last_w[r]); raw.add(last_w[r])
        for w in o["writes"]:
            if w in last_w:
                deps.add(last_w[w])
            for rd in readers.get(w, ()):
                deps.add(rd)
        deps.discard(i)
        keep = []
        for d in deps:
            de = ops[d]["eng"]
            if de == e:
                if e == "pe":
                    continue
                keep.append(d)
                continue
            keep.append(d)
        for w in o["writes"]:
            last_w[w] = i
            readers[w] = []
        for r in o["reads"]:
            if r not in o["writes"]:
                readers.setdefault(r, []).append(i)
        if e in DMA_ENGS:
            ch = o["chain"]
            assert ch is not None
            n = chain_seq.get(ch, 0)
            o["sig"] = (("c", ch), 16 * (n + 1))
            chain_seq[ch] = n + 1
        else:
            n = eng_seq.get(e, 0)
            o["sig"] = (("e", e), n + 1)
            eng_seq[e] = n + 1
        know = issue_know.setdefault(e, {})
        waits = []
        for d in sorted(keep, reverse=True):
            key, val = ops[d]["sig"]
            if know.get(key, 0) >= val:
                continue
            waits.append((key, val, ops[d]["clock"].get(("e", "pe"), 0)))
            kmax(know, ops[d]["clock"])
        o["waits"] = waits
        clk = dict(know)
        if e not in DMA_ENGS and e in prev_clock:
            kmax(clk, prev_clock[e])
        key, val = o["sig"]
        if clk.get(key, 0) < val:
            clk[key] = val
        o["clock"] = clk
        if e not in DMA_ENGS:
            prev_clock[e] = clk
    sems = {}

    def sem(key):
        if key not in sems:
            nm = "s_" + "_".join(str(x) for x in (key[1] if isinstance(key[1], tuple) else (key[1],)))
            sems[key] = stack.enter_context(nc.semaphore(nm))
        return sems[key]

    for o in ops:
        sem(o["sig"][0])
    block = stack.enter_context(nc.Block())

    def run(engname, eng):
        prev_free = []
        prev_sig = 0
        for o in ops:
            if o["eng"] != engname:
                continue
            waits = list(o["waits"])
            attach = None
            standalone = []
            hoisted = []
            if engname == "pe":
                for w in waits:
                    safe = w[2] < prev_sig
                    if safe and prev_free:
                        hoisted.append((prev_free.pop(), w))
                    elif attach is None:
                        attach = w
                    else:
                        standalone.append(w)
            elif engname not in DMA_ENGS and waits:
                attach = waits[0]
                standalone = waits[1:]
            else:
                standalone = waits
            for ins_, w in hoisted:
                ins_._wait_ge(sem(w[0]), w[1])
            for w in standalone:
                eng.wait_ge(sem(w[0]), w[1])
            ret = o["fn"](eng)
            if isinstance(ret, list):
                allins = ret
            elif isinstance(ret, tuple):
                allins = [ret[0], ret[1]] if ret[0] is not ret[1] else [ret[0]]
            else:
                allins = [ret]
            first, last = allins[0], allins[-1]
            if attach is not None:
                first._wait_ge(sem(attach[0]), attach[1])
            key, val = o["sig"]
            last.then_inc(sem(key), 16 if key[0] == "c" else 1)
            if engname == "pe":
                prev_free = allins[1:] if attach is not None else allins[:]
                prev_sig = val
        if engname == "sp":
            for ch in final_chains:
                if ch in chain_seq:
                    eng.wait_ge(sem(("c", ch)), 16 * chain_seq[ch])

    @block.sync
    def _(e):
        run("sp", e)

    @block.gpsimd
    def _(e):
        run("gq", e)

    @block.tensor
    def _(e):
        run("pe", e)

    @block.scalar
    def _(e):
        run("act", e)

    @block.vector
    def _(e):
        run("dve", e)


def build_nc():
    nc = bass.Bass("TRN2", target_bir_lowering=False)
    xT = nc.dram_tensor("xT", [D, XCOLS], F32, kind="ExternalInput").ap()
    memT = nc.dram_tensor("memT", [D, NMEM], F32, kind="ExternalInput").ap()
    wst = nc.dram_tensor("wst", [128, WTOT], F32, kind="ExternalInput").ap()
    cstd = nc.dram_tensor("cst", [128, NCST], F32, kind="ExternalInput").ap()
    oT = nc.dram_tensor("oT", [D, TOK], F32, kind="ExternalOutput").ap()

    stack = ExitStack()
    sb = lambda name, shape, dt: stack.enter_context(nc.sbuf_tensor(name, shape, dt))
    xbuf = sb("xbuf", [128, 16, T], F32)
    hbuf = sb("hbuf", [128, 16, T], BF16)
    ubuf = sb("ubuf", [128, 16, T], BF16)
    wring = sb("wring", [128, RING, SLOT], BF16)
    cst = sb("cstsb", [128, NCST], F32)
    ones = sb("ones", [128, 128], BF16)
    KT = sb("KT", [128, 4, NMEM], BF16)
    Vb = sb("Vb", [128, 2, 512], BF16)
    pwb = sb("pwb", [128, POOLW_SIZE], BF16)
    NSF = 10
    NSB = 6
    SW = 376
    SWB = 344
    scf = sb("scf", [128, NSF, SW], F32)
    scb = sb("scb", [128, NSB, SW], BF16)
    qb = sb("qb", [128, 3, SWB], BF16)
    eb = sb("eb", [128, 4, SWB], BF16)
    pb = sb("pb", [128, 6, SWB], BF16)
    acc = sb("acc", [128, 3, SWB], F32)
    ones_f = sb("ones_f", [128, 128], F32)
    psum = [stack.enter_context(nc.psum_tensor("ps%d" % b, [128, 512], F32)) for b in range(8)]

    P = Prog()
    cnt = {"f": 0, "b": 0, "slab": 0, "q": 0, "e": 0, "p": 0, "proj": 0, "u": 0, "d": 0}

    def rot(name, n):
        i = cnt[name] % n
        cnt[name] += 1
        return i

    new_f = lambda: rot("f", NSF)
    new_b = lambda: rot("b", NSB)
    proj_bank = lambda: rot("proj", 4)

    def cc(col):
        return cst[:, col:col + 1]

    def pw(pc, oc):
        i = PW_TILES.index((pc, oc))
        return pwb[:, i * 128:(i + 1) * 128]

    XB = [[("x", c, bi) for c in range(16)] for bi in range(3)]

    slab_reads = {}

    def load_slab(key):
        n = cnt["slab"]; cnt["slab"] += 1
        slot = n % RING
        size = _slab_size(key)
        off = W_OFF[key]
        P.op("gq", lambda e, slot=slot, size=size, off=off: e.dma_start(out=wring[:, slot, 0:size], in_=wst[:, off:off + size]),
             reads=slab_reads.get(n, []), writes=[("wr", slot)], chain=("wr", slot))
        return slot

    def mm_group(bank, ncols, pairs, reads):
        def fn(e, bank=bank, ncols=ncols, pairs=pairs):
            n = len(pairs)
            return [e.matmul(psum[bank][0:mrows, 0:ncols], lhsT=l, rhs=r, start=(i == 0), stop=(i == n - 1))
                    for i, (l, r, mrows) in enumerate(pairs)]
        P.op("pe", fn, reads=reads, writes=[("ps", bank)])

    def hres(t):
        r = [("h", k, t) for k in range(16)]
        if t > 0:
            r += [("h", k, t - 1) for k in range(16)]
        return r

    HALL = [("h", k, t) for k in range(16) for t in range(3)]

    def proj(slot, rows, lo, hi, t):
        bank = proj_bank()
        mm_group(bank, hi - lo, [(wring[:, slot, k * rows:(k + 1) * rows], hbuf[:, k, lo:hi], rows) for k in range(16)],
                 reads=[("wr", slot)] + hres(t))
        return bank

    def sq_accum(c, blk, bank, bi):
        s, e_ = blk
        w = e_ - s
        sq = new_b()
        P.op("act", lambda e, c=c, s=s, e_=e_, w=w, sq=sq: e.activation(out=scb[:, sq, 0:w], in_=xbuf[:, c, s:e_], func=AF.Square),
             reads=[("x", c, bi)], writes=[("scb", sq)])

        def pe_part():
            P.op("pe", lambda e, c=c, w=w, sq=sq, bank=bank: e.matmul(psum[bank][:, 0:w], lhsT=ones[:, :], rhs=scb[:, sq, 0:w], start=(c == 0), stop=(c == 15)),
                 reads=[("scb", sq), ("ones",)], writes=[("ps", bank)])
        return pe_part

    def sq_acc_dve(c, blk, t):
        s, e_ = blk
        w = e_ - s
        sq = new_b()
        P.op("act", lambda e, c=c, s=s, e_=e_, w=w, sq=sq: e.activation(out=scb[:, sq, 0:w], in_=xbuf[:, c, s:e_], func=AF.Square),
             reads=[("x", c, t)], writes=[("scb", sq)])

        def dve_part():
            if c == 0:
                P.op("dve", lambda e, w=w, sq=sq, t=t: e.tensor_copy(out=acc[:, t, 0:w], in_=scb[:, sq, 0:w]),
                     reads=[("scb", sq)], writes=[("acc", t)])
            else:
                P.op("dve", lambda e, w=w, sq=sq, t=t: e.tensor_tensor(out=acc[:, t, 0:w], in0=acc[:, t, 0:w], in1=scb[:, sq, 0:w], op=ALU.add),
                     reads=[("scb", sq), ("acc", t)], writes=[("acc", t)])
        return dve_part

    def stats_finish(blk, t, bank):
        w = blk[1] - blk[0]
        P.op("pe", lambda e, w=w, t=t, bank=bank: e.matmul(psum[bank][:, 0:w], lhsT=ones_f[:, :], rhs=acc[:, t, 0:w], start=True, stop=True),
             reads=[("acc", t), ("onesf",)], writes=[("ps", bank)])

    def rstd_of(blk, bank):
        s, e_ = blk
        w = e_ - s
        f = new_f()
        P.op("act", lambda e, w=w, f=f, bank=bank: e.activation(out=scf[:, f, 0:w], in_=psum[bank][:, 0:w], func=AF.Ln, bias=cc(C_EPS), scale=1.0 / D),
             reads=[("ps", bank), ("cst",)], writes=[("scf", f)])
        r = new_f()
        P.op("act", lambda e, w=w, f=f, r=r: e.activation(out=scf[:, r, 0:w], in_=scf[:, f, 0:w], func=AF.Exp, scale=-0.5),
             reads=[("scf", f)], writes=[("scf", r)])
        return r

    def norm_apply(c, blk, bi, r, gcol, to_h):
        s, e_ = blk
        w = e_ - s
        if to_h:
            P.op("dve", lambda e, c=c, s=s, e_=e_, w=w, r=r: e.scalar_tensor_tensor(out=hbuf[:, c, s:e_], in0=xbuf[:, c, s:e_], scalar=cc(gcol + c), in1=scf[:, r, 0:w], op0=ALU.mult, op1=ALU.mult),
                 reads=[("x", c, bi), ("scf", r), ("cst",)], writes=[("h", c, bi)])
        else:
            P.op("dve", lambda e, c=c, s=s, e_=e_, w=w, r=r: e.scalar_tensor_tensor(out=xbuf[:, c, s:e_], in0=xbuf[:, c, s:e_], scalar=cc(gcol + c), in1=scf[:, r, 0:w], op0=ALU.mult, op1=ALU.mult),
                 reads=[("x", c, bi), ("scf", r), ("cst",)], writes=[("x", c, bi)])

    NBANK = (4, 5, 6)
    NBANK_NEXT = (3, 4, 5)

    P.op("sp", lambda e: e.dma_start(out=cst[:, :], in_=cstd[:, :]), writes=[("cst",)], chain=("cst",))
    P.op("dve", lambda e: e.memset(ones[:, :], 1.0), writes=[("ones",)])
    P.op("dve", lambda e: e.memset(ones_f[:, :], 1.0), writes=[("onesf",)])
    P.op("gq", lambda e: e.dma_start(out=pwb[:, :], in_=wst[:, W_OFF["poolw"]:W_OFF["poolw"] + POOLW_SIZE]),
         writes=[("pwb",)], chain=("pwb",))
    memv = memT.rearrange("(c p) m -> c p m", p=128)
    xv = xT.rearrange("(c p) t -> c p t", p=128)
    ov = oT.rearrange("(c p) t -> c p t", p=128)
    mT = ubuf[:, 12:16, :].rearrange("p a b -> p (a b)")

    xv4 = xT.rearrange("(c p) t -> p c t", p=128)
    ov4 = oT.rearrange("(c p) t -> p c t", p=128)

    def load_x(job_, bi, cg, extra=()):
        c0_ = job_ * JNEW
        s, e_ = NB1[bi]
        P.op("sp", lambda e, cg=cg, c0_=c0_, s=s, e_=e_: e.dma_start(out=xbuf[:, 4 * cg:4 * cg + 4, s:e_], in_=xv4[:, 4 * cg:4 * cg + 4, c0_ + s:c0_ + e_]),
             reads=list(extra), writes=[("x", c, bi) for c in range(4 * cg, 4 * cg + 4)], chain=("x", cg, bi))

    def store_o(job_, t, cg):
        c0_ = job_ * JNEW
        a0, b0 = FT[t]
        P.op("sp", lambda e, cg=cg, c0_=c0_, a0=a0, b0=b0: e.dma_start(out=ov4[:, 4 * cg:4 * cg + 4, c0_ + a0 - HALO:c0_ + b0 - HALO], in_=xbuf[:, 4 * cg:4 * cg + 4, a0:b0]),
             reads=[("x", c, t) for c in range(4 * cg, 4 * cg + 4)], chain=("o", cg, t))

    for cg in range(4):
        load_x(0, 0, cg)
    slab_reads[0] = XB[0]
    slab_reads[6] = XB[2]

    def kv_prologue():
        MBANK = 7
        for c in range(16):
            f = new_f()
            P.op("sp", lambda e, c=c, f=f: e.dma_start(out=scf[:, f, 0:NMEM], in_=memv[c]), writes=[("scf", f)], chain=("scfd", f))
            sq = new_b()
            P.op("act", lambda e, f=f, sq=sq: e.activation(out=scb[:, sq, 0:NMEM], in_=scf[:, f, 0:NMEM], func=AF.Square),
                 reads=[("scf", f)], writes=[("scb", sq)])
            P.op("pe", lambda e, c=c, sq=sq: e.matmul(psum[MBANK][:, 0:NMEM], lhsT=ones[:, :], rhs=scb[:, sq, 0:NMEM], start=(c == 0), stop=(c == 15)),
                 reads=[("scb", sq), ("ones",)], writes=[("ps", MBANK)])
        rm = rstd_of((0, NMEM), MBANK)
        rmk = acc[:, 0, 0:NMEM]
        P.op("act", lambda e: e.activation(out=rmk, in_=scf[:, rm, 0:NMEM], func=AF.Copy), reads=[("scf", rm)], writes=[("acc", 0)])
        for c in range(16):
            f = new_f()
            P.op("sp", lambda e, c=c, f=f: e.dma_start(out=scf[:, f, 0:NMEM], in_=memv[c]), writes=[("scf", f)], chain=("scfd", f))
            P.op("dve", lambda e, c=c, f=f: e.scalar_tensor_tensor(out=mT[:, c * NMEM:(c + 1) * NMEM], in0=scf[:, f, 0:NMEM], scalar=cc(C_GMEM + c), in1=rmk, op0=ALU.mult, op1=ALU.mult),
                 reads=[("scf", f), ("acc", 0), ("cst",)], writes=[("u", 12), ("u", 13), ("u", 14), ("u", 15)])
        MTR = [("u", 12), ("u", 13), ("u", 14), ("u", 15)]
        for h in range(4):
            slot = load_slab(("kh", h))
            bank = proj_bank()
            mm_group(bank, NMEM, [(wring[:, slot, k * 128:(k + 1) * 128], mT[:, k * NMEM:(k + 1) * NMEM], 128) for k in range(16)],
                     reads=[("wr", slot)] + MTR)
            P.op("act", lambda e, h=h, bank=bank: e.activation(out=KT[:, h, :], in_=psum[bank][:, 0:NMEM], func=AF.Copy),
                 reads=[("ps", bank)], writes=[("KT",)])
        for h in range(4):
            slot = load_slab(("vh", h))
            for mc in range(2):
                bank = proj_bank()
                mm_group(bank, 128, [(mT[:, k * NMEM + mc * 128:k * NMEM + (mc + 1) * 128], wring[:, slot, k * 128:(k + 1) * 128], 128) for k in range(16)],
                         reads=[("wr", slot)] + MTR)
                P.op("act", lambda e, h=h, mc=mc, bank=bank: e.activation(out=Vb[:, mc, h * 128:(h + 1) * 128], in_=psum[bank][:, 0:128], func=AF.Copy),
                     reads=[("ps", bank)], writes=[("Vb",)])

    def stats_block(bi):
        for c in range(16):
            sq_accum(c, NB1[bi], NBANK[bi], bi)()

    def norm1_block(bi):
        r_ = rstd_of(NB1[bi], NBANK[bi])
        for c in range(16):
            norm_apply(c, NB1[bi], bi, r_, C_GMIX, True)

    for job in range(NJOB):
        c0 = job * JNEW

        def att_head(h):
            slot = load_slab(("q", h))
            qs = {}
            es = {}

            def U(t):
                m0, b0 = MT[t]
                w = b0 - m0
                bank = proj(slot, 128, m0, b0, t)
                q = rot("q", 3)
                P.op("act", lambda e, w=w, q=q, bank=bank: e.activation(out=qb[:, q, 0:w], in_=psum[bank][:, 0:w], func=AF.Copy),
                     reads=[("ps", bank)], writes=[("qb", q)])
                qs[t] = q

            def S(t):
                m0, b0 = MT[t]
                w = b0 - m0
                et = []
                for mc in range(2):
                    bank = 4 + 2 * (t % 2) + mc
                    mm_group(bank, w, [(KT[:, h, mc * 128:(mc + 1) * 128], qb[:, qs[t], 0:w], 128)],
                             reads=[("KT",), ("qb", qs[t])])
                    ei = rot("e", 4)
                    P.op("act", lambda e, w=w, ei=ei, bank=bank: e.activation(out=eb[:, ei, 0:w], in_=psum[bank][:, 0:w], func=AF.Exp, scale=float(ATT_SCALE)),
                         reads=[("ps", bank)], writes=[("eb", ei)])
                    et.append(ei)
                es[t] = et

            def A(t):
                m0, b0 = MT[t]
                w = b0 - m0
                ba = proj_bank()
                bs = proj_bank()
                mm_group(ba, w, [(Vb[:, mc, h * 128:(h + 1) * 128], eb[:, es[t][mc], 0:w], 128) for mc in range(2)],
                         reads=[("Vb",)] + [("eb", es[t][mc]) for mc in range(2)])
                mm_group(bs, w, [(ones[:, :], eb[:, es[t][mc], 0:w], 128) for mc in range(2)],
                         reads=[("ones",)] + [("eb", es[t][mc]) for mc in range(2)])
                fl = new_f()
                P.op("act", lambda e, w=w, fl=fl, bs=bs: e.activation(out=scf[:, fl, 0:w], in_=psum[bs][:, 0:w], func=AF.Ln),
                     reads=[("ps", bs)], writes=[("scf", fl)])
                f = new_f()
                P.op("act", lambda e, w=w, fl=fl, f=f: e.activation(out=scf[:, f, 0:w], in_=scf[:, fl, 0:w], func=AF.Exp, scale=-1.0),
                     reads=[("scf", fl)], writes=[("scf", f)])
                P.op("dve", lambda e, w=w, f=f, ba=ba, m0=m0, b0=b0: e.tensor_tensor(out=ubuf[:, 12 + h, m0:b0], in0=psum[ba][:, 0:w], in1=scf[:, f, 0:w], op=ALU.mult),
                     reads=[("ps", ba), ("scf", f)], writes=[("u", 12 + h)])

            U(0); U(1); S(0); U(2); S(1); A(0); S(2); A(1); A(2)

        lin_q = []

        PC_WIN = {0: (2, 2), 1: (2, 4), 2: (4, 4), 3: (8, 8), 4: (8, 16), 5: (16, 16)}

        def pool_step(pc, t, slot):
            klo, khi = PC_WIN[pc]
            m0, b0 = MT[t]
            w = b0 - m0
            wn = w + 15
            bank = proj(slot, 128, m0 - 15, b0, t)
            v = new_f()
            P.op("act", lambda e, wn=wn, v=v, bank=bank: e.activation(out=scf[:, v, 0:wn], in_=psum[bank][:, 0:wn], func=AF.Copy),
                 reads=[("ps", bank)], writes=[("scf", v)])
            sums = {1: v}
            cur = v
            sh = 1
            while sh < khi:
                nx = new_f()
                P.op("dve", lambda e, wn=wn, cur=cur, nx=nx, sh=sh: e.tensor_tensor(out=scf[:, nx, sh:wn], in0=scf[:, cur, sh:wn], in1=scf[:, cur, 0:wn - sh], op=ALU.add),
                     reads=[("scf", cur)], writes=[("scf", nx)])
                cur = nx
                sh *= 2
                sums[sh] = cur
            pi = rot("p", 6)
            halves = [(0, 128, klo)] if klo == khi else [(0, 64, klo), (64, 128, khi)]
            for (r0, r1, kk) in halves:
                sk = sums[kk]
                wi = POOL_K.index(kk)
                if job == 0 and t == 0:
                    P.op("dve", lambda e, r0=r0, r1=r1, sk=sk, wi=wi: e.tensor_tensor(out=scf[r0:r1, sk, 17:33], in0=scf[r0:r1, sk, 17:33], in1=cst[r0:r1, C_CORR + 16 * wi:C_CORR + 16 * wi + 16], op=ALU.mult),
                         reads=[("scf", sk), ("cst",)], writes=[("scf", sk)])
                P.op("dve", lambda e, r0=r0, r1=r1, w=w, sk=sk, v=v, pi=pi, kk=kk: e.scalar_tensor_tensor(out=pb[r0:r1, pi, 0:w], in0=scf[r0:r1, sk, 15:15 + w], scalar=1.0 / kk, in1=scf[r0:r1, v, 15:15 + w], op0=ALU.mult, op1=ALU.subtract),
                     reads=[("scf", sk), ("scf", v)], writes=[("pb", pi)])
            return pi

        def pool_linear(blk, t, pooled):
            m0, b0 = MT[t]
            w = b0 - m0
            for oc in range(3 * blk, 3 * blk + 3):
                ks = [pc for (pc, o_) in PW_TILES if o_ == oc]
                bank = proj_bank()
                mm_group(bank, w, [(pw(pc, oc), pb[:, pooled[pc - 3 * blk], 0:w], 128) for pc in ks],
                         reads=[("pwb",)] + [("pb", pooled[pc - 3 * blk]) for pc in ks])
                P.op("act", lambda e, bank=bank, oc=oc, m0=m0, b0=b0, w=w: e.activation(out=ubuf[:, 6 + oc, m0:b0], in_=psum[bank][:, 0:w], func=AF.Identity, scale=cc(C_POOLSC + oc)),
                     reads=[("ps", bank), ("cst",)], writes=[("u", 6 + oc)])

        def conv_tile(i, t, scx, scc, scbk):
            m0, b0 = MT[t]
            w = b0 - m0
            wn = w + 2
            bx = proj(scx, 128, m0 - 2, b0, t)
            fx = new_f()
            P.op("act", lambda e, wn=wn, fx=fx, bx=bx: e.activation(out=scf[:, fx, 0:wn], in_=psum[bx][:, 0:wn], func=AF.Copy),
                 reads=[("ps", bx)], writes=[("scf", fx)])
            bc = proj(scc, 128, m0 - 2, b0, t)
            fv = new_f()
            P.op("dve", lambda e, wn=wn, fx=fx, fv=fv, bc=bc: e.tensor_tensor(out=scf[:, fv, 0:wn], in0=psum[bc][:, 0:wn], in1=scf[:, fx, 0:wn], op=ALU.mult),
                 reads=[("ps", bc), ("scf", fx)], writes=[("scf", fv)])
            fc = new_f()
            P.op("act", lambda e, w=w, fv=fv, fc=fc, i=i: e.activation(out=scf[:, fc, 0:w], in_=scf[:, fv, 2:2 + w], func=AF.Identity, scale=cc(C_CONVW + 3 * i + 2)),
                 reads=[("scf", fv), ("cst",)], writes=[("scf", fc)])
            bb = proj(scbk, 128, m0, b0, t)
            P.op("dve", lambda e, w=w, fv=fv, fc=fc, i=i: e.scalar_tensor_tensor(out=scf[:, fc, 0:w], in0=scf[:, fv, 1:1 + w], scalar=cc(C_CONVW + 3 * i + 1), in1=scf[:, fc, 0:w], op0=ALU.mult, op1=ALU.add),
                 reads=[("scf", fv), ("scf", fc), ("cst",)], writes=[("scf", fc)])
            fc2 = new_f()
            P.op("dve", lambda e, w=w, fv=fv, fc=fc, fc2=fc2, i=i: e.scalar_tensor_tensor(out=scf[:, fc2, 0:w], in0=scf[:, fv, 0:w], scalar=cc(C_CONVW + 3 * i + 0), in1=scf[:, fc, 0:w], op0=ALU.mult, op1=ALU.add),
                 reads=[("scf", fv), ("scf", fc), ("cst",)], writes=[("scf", fc2)])
            P.op("dve", lambda e, w=w, fc2=fc2, bb=bb, i=i, m0=m0, b0=b0: e.tensor_tensor(out=ubuf[:, i, m0:b0], in0=psum[bb][:, 0:w], in1=scf[:, fc2, 0:w], op=ALU.mult),
                 reads=[("ps", bb), ("scf", fc2)], writes=[("u", i)])

        def conv_head(i):
            scx = load_slab(("cx", i))
            scc = load_slab(("cc", i))
            scbk = load_slab(("cb", i))
            for t in range(3):
                conv_tile(i, t, scx, scc, scbk)
                if lin_q:
                    lin_q.pop(0)()

        if job == 0:
            stats_block(0)
            norm1_block(0)
            sl0 = [load_slab(("cx", 0)), load_slab(("cc", 0)), load_slab(("cb", 0))]
            sl1 = [load_slab(("cx", 1)), load_slab(("cc", 1)), load_slab(("cb", 1))]
            for bi in (1, 2):
                for cg in range(4):
                    load_x(0, bi, cg, extra=[("wr", sl1[2])])
        else:
            sl0 = [load_slab(("cx", 0)), load_slab(("cc", 0)), load_slab(("cb", 0))]
            sl1 = [load_slab(("cx", 1)), load_slab(("cc", 1)), load_slab(("cb", 1))]
        conv_tile(0, 0, *sl0)
        conv_tile(1, 0, *sl1)
        stats_block(1)
        norm1_block(1)
        stats_block(2)
        conv_tile(0, 1, *sl0)
        norm1_block(2)
        conv_tile(1, 1, *sl1)
        conv_tile(0, 2, *sl0)
        conv_tile(1, 2, *sl1)
        conv_head(2)
        for blk in (1, 0):
            pslots = [load_slab(("pv", 3 * blk + i)) for i in range(3)]
            for t in range(3):
                pooled = [pool_step(3 * blk + i, t, pslots[i]) for i in range(3)]
                if lin_q:
                    lin_q.pop(0)()
                lin_q.append(lambda blk=blk, t=t, pooled=pooled: pool_linear(blk, t, pooled))
        if job == 0:
            kv_prologue()
        for h in range(4):
            att_head(h)
            if lin_q:
                lin_q.pop(0)()
        for i in (3, 4, 5):
            conv_head(i)
        while lin_q:
            lin_q.pop(0)()

        def wout_group(m, t, s0):
            m0, b0 = MT[t]
            w = b0 - m0
            bank = proj_bank()
            prs = [(wring[:, s0, kt * 128:(kt + 1) * 128], ubuf[:, kt, m0:b0], 128) for kt in range(16)]
            mm_group(bank, w, prs, reads=[("wr", s0)] + [("u", kt) for kt in range(16)])
            P.op("dve", lambda e, w=w, m=m, bank=bank, m0=m0, b0=b0: e.tensor_tensor(out=xbuf[:, m, m0:b0], in0=psum[bank][:, 0:w], in1=xbuf[:, m, m0:b0], op=ALU.add),
                 reads=[("ps", bank), ("x", m, t)], writes=[("x", m, t)])
            return sq_acc_dve(m, MT[t], t)

        def norm2_block(bi):
            stats_finish(MT[bi], bi, NBANK[bi])
            r_ = rstd_of(MT[bi], NBANK[bi])
            for c in range(16):
                norm_apply(c, MT[bi], bi, r_, C_GFFN, True)
            if job == 0 and bi == 0:
                P.op("dve", lambda e: e.tensor_scalar(out=hbuf[:, :, 30:32], in0=hbuf[:, :, 30:32], scalar1=cc(C_HMASK), scalar2=None, op0=ALU.mult),
                     reads=[("h", k, 0) for k in range(16)] + [("cst",)], writes=[("h", k, 0) for k in range(16)])

        pend = []
        for m in range(14):
            s0 = load_slab(("wout", m))
            newp = [wout_group(m, t, s0) for t in range(3)]
            for p_ in pend:
                p_()
            pend = newp
        sA = load_slab(("wout", 14))
        sB = load_slab(("wout", 15))
        for t in range(3):
            for (m, s_) in ((14, sA), (15, sB)):
                newp = [wout_group(m, t, s_)]
                for p_ in pend:
                    p_()
                pend = newp
            for p_ in pend:
                p_()
            pend = []
            norm2_block(t)

        def up_pair(j):
            sg_ = load_slab(("upg", j))
            sv_ = load_slab(("upv", j))
            for t in range(3):
                up_unit(j, t, sg_, sv_)

        def up_unit(j, t, sg_, sv_):
            aslot = j % ARING
            a0, b0 = FT[t]
            if True:
                w = b0 - a0
                wn = w + 2
                u = rot("u", 3)
                bg, bv = 2 * u, 2 * u + 1
                def fn(e, bg=bg, bv=bv, wn=wn, sg_=sg_, sv_=sv_, a0=a0, b0=b0):
                    out_ = []
                    for bank_, sl_ in ((bg, sg_), (bv, sv_)):
                        for k in range(16):
                            out_.append(e.matmul(psum[bank_][:, 0:wn], lhsT=wring[:, sl_, k * 128:(k + 1) * 128], rhs=hbuf[:, k, a0 - 2:b0], start=(k == 0), stop=(k == 15)))
                    return out_
                P.op("pe", fn, reads=[("wr", sg_), ("wr", sv_)] + hres(t), writes=[("ps", bg), ("ps", bv)])
                tg = new_f(); tv = new_f(); sgb = new_f()
                wg = C_FFNW + 3 * j
                wv = C_FFNW + 3 * (NPAIR + j)
                P.op("act", lambda e, w=w, tg=tg, bg=bg, wg=wg, j=j: e.activation(out=scf[:, tg, 0:w], in_=psum[bg][:, 2:2 + w], func=AF.Identity, bias=cc(C_FFNB + j), scale=cc(wg + 2)),
                     reads=[("ps", bg), ("cst",)], writes=[("scf", tg)])
                P.op("act", lambda e, w=w, tv=tv, bv=bv, wv=wv, j=j: e.activation(out=scf[:, tv, 0:w], in_=psum[bv][:, 2:2 + w], func=AF.Identity, bias=cc(C_FFNB + NPAIR + j), scale=cc(wv + 2)),
                     reads=[("ps", bv), ("cst",)], writes=[("scf", tv)])
                P.op("dve", lambda e, w=w, tg=tg, bg=bg, wg=wg: e.scalar_tensor_tensor(out=scf[:, tg, 0:w], in0=psum[bg][:, 1:1 + w], scalar=cc(wg + 1), in1=scf[:, tg, 0:w], op0=ALU.mult, op1=ALU.add),
                     reads=[("ps", bg), ("scf", tg), ("cst",)], writes=[("scf", tg)])
                P.op("dve", lambda e, w=w, tv=tv, bv=bv, wv=wv: e.scalar_tensor_tensor(out=scf[:, tv, 0:w], in0=psum[bv][:, 1:1 + w], scalar=cc(wv + 1), in1=scf[:, tv, 0:w], op0=ALU.mult, op1=ALU.add),
                     reads=[("ps", bv), ("scf", tv), ("cst",)], writes=[("scf", tv)])
                P.op("dve", lambda e, w=w, tg=tg, bg=bg, wg=wg: e.scalar_tensor_tensor(out=scf[:, tg, 0:w], in0=psum[bg][:, 0:w], scalar=cc(wg), in1=scf[:, tg, 0:w], op0=ALU.mult, op1=ALU.add),
                     reads=[("ps", bg), ("scf", tg), ("cst",)], writes=[("scf", tg)])
                P.op("dve", lambda e, w=w, tv=tv, bv=bv, wv=wv: e.scalar_tensor_tensor(out=scf[:, tv, 0:w], in0=psum[bv][:, 0:w], scalar=cc(wv), in1=scf[:, tv, 0:w], op0=ALU.mult, op1=ALU.add),
                     reads=[("ps", bv), ("scf", tv), ("cst",)], writes=[("scf", tv)])
                P.op("act", lambda e, w=w, tg=tg, sgb=sgb: e.activation(out=scf[:, sgb, 0:w], in_=scf[:, tg, 0:w], func=AF.Silu),
                     reads=[("scf", tg)], writes=[("scf", sgb)])
                P.op("dve", lambda e, w=w, tv=tv, sgb=sgb, aslot=aslot, a0=a0, b0=b0: e.tensor_tensor(out=ubuf[:, aslot, a0:b0], in0=scf[:, sgb, 0:w], in1=scf[:, tv, 0:w], op=ALU.mult),
                     reads=[("scf", sgb), ("scf", tv)], writes=[("u", aslot)])

        FBANK = (0, 1, 2)

        def down_group(g):
            pend = []
            for m in range(16):
                slot = load_slab(("down", g, m))
                newp = []
                for t, (a0, b0) in enumerate(FT):
                    w = b0 - a0
                    bank = 6 + rot("d", 2)
                    prs = [(wring[:, slot, kk * 128:(kk + 1) * 128], ubuf[:, (g * GK + kk) % ARING, a0:b0], 128) for kk in range(GK)]
                    mm_group(bank, w, prs, reads=[("wr", slot)] + [("u", (g * GK + kk) % ARING) for kk in range(GK)])
                    P.op("dve", lambda e, w=w, m=m, bank=bank, a0=a0, b0=b0: e.tensor_tensor(out=xbuf[:, m, a0:b0], in0=psum[bank][:, 0:w], in1=xbuf[:, m, a0:b0], op=ALU.add),
                         reads=[("ps", bank), ("x", m, t)], writes=[("x", m, t)])
                    if g == NGRP - 1:
                        newp.append(sq_acc_dve(m, FT[t], t))
                for p_ in pend:
                    p_()
                pend = newp
            for p_ in pend:
                p_()

        ps0 = (load_slab(("upg", 0)), load_slab(("upv", 0)))
        ps1 = (load_slab(("upg", 1)), load_slab(("upv", 1)))
        for t in range(3):
            up_unit(0, t, *ps0)
            up_unit(1, t, *ps1)
        done = 2
        for g in range(NGRP):
            hi = min(NPAIR, GK * (g + 1) + 1)
            for j in range(done, hi):
                up_pair(j)
            done = hi
            down_group(g)

        nxt = job + 1 < NJOB
        for bi, blk in enumerate(FT):
            stats_finish(blk, bi, FBANK[bi])
        rs_ = [rstd_of(blk, FBANK[bi]) for bi, blk in enumerate(FT)]
        for t in range(3):
            if t == 2 and nxt:
                stats_block(0)
                norm1_block(0)
            for c in range(16):
                norm_apply(c, FT[t], t, rs_[t], C_GFIN, False)
                if c % 4 == 3:
                    cg = c // 4
                    store_o(job, t, cg)
                    if nxt and cg >= 1:
                        load_x(job + 1, t, cg - 1)
            if nxt:
                load_x(job + 1, t, 3)

    _emit(nc, P, stack, [("o", cg, t) for cg in range(4) for t in range(3)])
    stack.close()
    return nc


_NC_CACHE = {}


def kernel(x, mem, g_mix, g_mem, w_in, conv_w, pool_w, pool_scale, w_kv, w_out,
           g_ffn, w_up, ffn_conv_w, ffn_conv_b, w_down, g_final):
    f = lambda a: np.asarray(a, dtype=np.float32)
    x = f(x)[0]; mem = f(mem)[0]
    g_mix = f(g_mix)[0]; g_mem = f(g_mem)[0]; g_ffn = f(g_ffn)[0]; g_final = f(g_final)
    w_in = f(w_in)[0]; conv_w = f(conv_w)[0]; pool_w = f(pool_w)[0]; pool_scale = f(pool_scale)[0]
    w_kv = f(w_kv)[0]; w_out = f(w_out)[0]; w_up = f(w_up)[0]
    ffn_conv_w = f(ffn_conv_w)[0]; ffn_conv_b = f(ffn_conv_b)[0]; w_down = f(w_down)[0]

    ws = _build_wstream(w_in, pool_w, w_kv, w_out, w_up, w_down)
    memT = np.ascontiguousarray(mem.T)
    in_maps = []
    for c in range(NCORES):
        xt = np.zeros((D, XCOLS), np.float32)
        lo = c * TOK - HALO
        if lo < 0:
            xt[:, HALO:] = x[0:TOK].T
        else:
            xt[:, :] = x[lo:lo + XCOLS].T
        cst = _build_consts(c, g_mix, g_mem, g_ffn, g_final, conv_w, pool_scale, ffn_conv_w, ffn_conv_b)
        in_maps.append({"xT": xt, "memT": memT, "wst": ws, "cst": cst})
    if "nc" not in _NC_CACHE:
        _NC_CACHE["nc"] = build_nc()
    nc = _NC_CACHE["nc"]
    res = run_bass_kernel_spmd(nc, in_maps, core_ids=list(range(NCORES)))
    out = np.empty((1, SEQ, D), np.float32)
    for c in range(NCORES):
        out[0, c * TOK:(c + 1) * TOK, :] = res.results[c]["oT"].T
    return out
```

```python
import numpy as np
from contextlib import ExitStack
import concourse.bass as bass
import concourse.mybir as mybir
from concourse.bass_utils import run_bass_kernel_spmd

F32 = mybir.dt.float32
BF16 = mybir.dt.bfloat16
AF = mybir.ActivationFunctionType
ALU = mybir.AluOpType

NCORES = 8
D = 2048
SEQ = 16384
TOK = SEQ // NCORES
HALO = 32
NJOB = 2
JNEW = TOK // NJOB
T = JNEW + HALO
XCOLS = TOK + HALO
NMEM = 256
DFF = 5632
NPAIR = DFF // 128
GK = 11
NGRP = NPAIR // GK
ARING = 13
EPS = 1e-6
ATT_SCALE = 1.0 / np.sqrt(128.0)
POOL_K = (2, 4, 8, 16)

MT = [(30, 374), (374, 715), (715, 1056)]
FT = [(32, 374), (374, 715), (715, 1056)]
NB1 = [(0, 374), (374, 715), (715, 1056)]

RING = 8
XLAG = 4
SLOT = 16 * 128

C_GMIX = 0
C_GMEM = 16
C_GFFN = 32
C_GFIN = 48
C_CONVW = 64
C_POOLSC = C_CONVW + 18
C_FFNW = C_POOLSC + 8
C_FFNB = C_FFNW + 264
C_CORR = C_FFNB + 88
C_HMASK = C_CORR + 64
C_EPS = C_HMASK + 1
NCST = C_EPS + 1


def _job_slabs():
    s = []
    for h in range(4):
        s.append(("q", h))
    for pc in (3, 4, 5, 0, 1, 2):
        s.append(("pv", pc))
    for i in range(6):
        s.append(("cx", i)); s.append(("cc", i)); s.append(("cb", i))
    for m in range(16):
        s.append(("wout", m))
    done = 0
    for g in range(NGRP):
        hi = min(NPAIR, GK * (g + 1) + 1)
        for j in range(done, hi):
            s.append(("upg", j)); s.append(("upv", j))
        done = hi
        for m in range(16):
            s.append(("down", g, m))
    return s


def _slab_size(s):
    k = s[0]
    if k in ("q", "cx", "cc", "cb", "upg", "upv", "kh", "vh", "pv", "wout"):
        return 16 * 128
    if k == "down":
        return GK * 128
    raise ValueError(k)


KV_SLABS = [("kh", h) for h in range(4)] + [("vh", h) for h in range(4)]
JOB_SLABS = _job_slabs()
PW_TILES = [(0, 0), (1, 0), (0, 1), (1, 1), (2, 1), (1, 2), (2, 2),
            (3, 3), (4, 3), (3, 4), (4, 4), (5, 4), (4, 5), (5, 5)]
POOLW_SIZE = len(PW_TILES) * 128
W_OFF = {}
_o = 0
for _s in KV_SLABS:
    W_OFF[_s] = _o; _o += _slab_size(_s)
W_OFF["poolw"] = _o; _o += POOLW_SIZE
for _s in JOB_SLABS:
    W_OFF[_s] = _o; _o += _slab_size(_s)
WTOT = _o


def _kslab(w, c0, nc_):
    K = w.shape[0]
    a = w[:, c0:c0 + nc_].reshape(K // 128, 128, nc_).transpose(1, 0, 2)
    return np.ascontiguousarray(a).reshape(128, -1)


def _build_wstream(w_in, pool_w, w_kv, w_out, w_up, w_down):
    ws = np.zeros((128, WTOT), np.float32)

    def put(key, arr):
        o = W_OFF[key]
        ws[:, o:o + arr.shape[1]] = arr

    for h in range(4):
        put(("kh", h), _kslab(w_kv, h * 128, 128))
        put(("vh", h), _kslab(w_kv, 512 + h * 128, 128))
    PW = np.zeros((768, 768), np.float32)
    for g in range(4):
        PW[192 * g:192 * g + 192, 192 * g:192 * g + 192] = pool_w[g]
    pw = np.zeros((128, len(PW_TILES), 128), np.float32)
    for i, (pc, oc) in enumerate(PW_TILES):
        pw[:, i, :] = PW[128 * pc:128 * pc + 128, 128 * oc:128 * oc + 128]
    put("poolw", pw.reshape(128, -1))
    for s in JOB_SLABS:
        k = s[0]
        if k == "q":
            put(s, _kslab(w_in, 3072 + s[1] * 128, 128))
        elif k == "pv":
            put(s, _kslab(w_in, 2304 + 128 * s[1], 128))
        elif k == "cb":
            put(s, _kslab(w_in, s[1] * 128, 128))
        elif k == "cc":
            put(s, _kslab(w_in, 768 + s[1] * 128, 128))
        elif k == "cx":
            put(s, _kslab(w_in, 1536 + s[1] * 128, 128))
        elif k == "wout":
            put(s, _kslab(w_out, s[1] * 128, 128))
        elif k == "upg":
            put(s, _kslab(w_up, s[1] * 128, 128))
        elif k == "upv":
            put(s, _kslab(w_up, DFF + s[1] * 128, 128))
        elif k == "down":
            g, m = s[1], s[2]
            a = w_down[g * GK * 128:(g + 1) * GK * 128, m * 128:(m + 1) * 128]
            a = a.reshape(GK, 128, 128).transpose(1, 0, 2)
            put(s, np.ascontiguousarray(a).reshape(128, -1))
    return ws


def _vec16(v):
    return np.ascontiguousarray(v.reshape(-1, 128).T)


def _build_consts(core, g_mix, g_mem, g_ffn, g_final, conv_w, pool_scale, ffn_conv_w, ffn_conv_b):
    c = np.zeros((128, NCST), np.float32)
    c[:, C_GMIX:C_GMIX + 16] = _vec16(g_mix)
    c[:, C_GMEM:C_GMEM + 16] = _vec16(g_mem)
    c[:, C_GFFN:C_GFFN + 16] = _vec16(g_ffn)
    c[:, C_GFIN:C_GFIN + 16] = _vec16(g_final)
    for i in range(6):
        for j in range(3):
            c[:, C_CONVW + i * 3 + j] = conv_w[j, i * 128:(i + 1) * 128]
    for oc in range(6):
        c[:, C_POOLSC + oc] = pool_scale[128 * oc:128 * oc + 128]
    for ch in range(88):
        for j in range(3):
            c[:, C_FFNW + ch * 3 + j] = ffn_conv_w[j, ch * 128:(ch + 1) * 128]
        c[:, C_FFNB + ch] = ffn_conv_b[ch * 128:(ch + 1) * 128]
    for w, k in enumerate(POOL_K):
        for i in range(16):
            c[:, C_CORR + w * 16 + i] = (float(k) / float(min(i + 1, k))) if core == 0 else 1.0
    c[:, C_HMASK] = 0.0 if core == 0 else 1.0
    c[:, C_EPS] = EPS
    return c


class Prog:
    def __init__(self):
        self.ops = []

    def op(self, eng, fn, reads=(), writes=(), chain=None):
        self.ops.append(dict(eng=eng, fn=fn, reads=tuple(reads), writes=tuple(writes), chain=chain))
        return len(self.ops) - 1


def _emit(nc, prog, stack, final_chains):
    ops = prog.ops
    DMA_ENGS = ("sp", "gq")
    last_w = {}
    readers = {}
    eng_seq = {}
    chain_seq = {}
    issue_know = {}
    prev_clock = {}

    def kmax(dst, src_):
        for k, v in src_.items():
            if dst.get(k, 0) < v:
                dst[k] = v

    for i, o in enumerate(ops):
        e = o["eng"]
        deps = set()
        raw = set()
        for r in o["reads"]:
            if r in last_w:
                deps.add(last_w[r]); raw.add(last_w[r])
        for w in o["writes"]:
            if w in last_w:
                deps.add(last_w[w])
            for rd in readers.get(w, ()):
                deps.add(rd)
        deps.discard(i)
        keep = []
        for d in deps:
            de = ops[d]["eng"]
            if de == e:
                if e == "pe":
                    continue
                keep.append(d)
                continue
            keep.append(d)
        for w in o["writes"]:
            last_w[w] = i
            readers[w] = []
        for r in o["reads"]:
            if r not in o["writes"]:
                readers.setdefault(r, []).append(i)
        if e in DMA_ENGS:
            ch = o["chain"]
            assert ch is not None
            n = chain_seq.get(ch, 0)
            o["sig"] = (("c", ch), 16 * (n + 1))
            chain_seq[ch] = n + 1
        else:
            n = eng_seq.get(e, 0)
            o["sig"] = (("e", e), n + 1)
            eng_seq[e] = n + 1
        know = issue_know.setdefault(e, {})
        waits = []
        for d in sorted(keep, reverse=True):
            key, val = ops[d]["sig"]
            if know.get(key, 0) >= val:
                continue
            waits.append((key, val, ops[d]["clock"].get(("e", "pe"), 0)))
            kmax(know, ops[d]["clock"])
        o["waits"] = waits
        clk = dict(know)
        if e not in DMA_ENGS and e in prev_clock:
            kmax(clk, prev_clock[e])
        key, val = o["sig"]
        if clk.get(key, 0) < val:
            clk[key] = val
        o["clock"] = clk
        if e not in DMA_ENGS:
            prev_clock[e] = clk
    sems = {}

    def sem(key):
        if key not in sems:
            nm = "s_" + "_".join(str(x) for x in (key[1] if isinstance(key[1], tuple) else (key[1],)))
            sems[key] = stack.enter_context(nc.semaphore(nm))
        return sems[key]

    for o in ops:
        sem(o["sig"][0])
    block = stack.enter_context(nc.Block())

    def run(engname, eng):
        prev_free = []
        prev_sig = 0
        for o in ops:
            if o["eng"] != engname:
                continue
            waits = list(o["waits"])
            attach = None
            standalone = []
            hoisted = []
            if engname == "pe":
                for w in waits:
                    safe = w[2] < prev_sig
                    if safe and prev_free:
                        hoisted.append((prev_free.pop(), w))
                    elif attach is None:
                        attach = w
                    else:
                        standalone.append(w)
            elif engname not in DMA_ENGS and waits:
                attach = waits[0]
                standalone = waits[1:]
            else:
                standalone = waits
            for ins_, w in hoisted:
                ins_._wait_ge(sem(w[0]), w[1])
            for w in standalone:
                eng.wait_ge(sem(w[0]), w[1])
            ret = o["fn"](eng)
            if isinstance(ret, list):
                allins = ret
            elif isinstance(ret, tuple):
                allins = [ret[0], ret[1]] if ret[0] is not ret[1] else [ret[0]]
            else:
                allins = [ret]
            first, last = allins[0], allins[-1]
            if attach is not None:
                first._wait_ge(sem(attach[0]), attach[1])
            key, val = o["sig"]
            last.then_inc(sem(key), 16 if key[0] == "c" else 1)
            if engname == "pe":
                prev_free = allins[1:] if attach is not None else allins[:]
                prev_sig = val
        if engname == "sp":
            for ch in final_chains:
                if ch in chain_seq:
                    eng.wait_ge(sem(("c", ch)), 16 * chain_seq[ch])

    @block.sync
    def _(e):
        run("sp", e)

    @block.gpsimd
    def _(e):
        run("gq", e)

    @block.tensor
    def _(e):
        run("pe", e)

    @block.scalar
    def _(e):
        run("act", e)

    @block.vector
    def _(e):
        run("dve", e)


def build_nc():
    nc = bass.Bass("TRN2", target_bir_lowering=False)
    xT = nc.dram_tensor("xT", [D, XCOLS], F32, kind="ExternalInput").ap()
    memT = nc.dram_tensor("memT", [D, NMEM], F32, kind="ExternalInput").ap()
    wst = nc.dram_tensor("wst", [128, WTOT], F32, kind="ExternalInput").ap()
    cstd = nc.dram_tensor("cst", [128, NCST], F32, kind="ExternalInput").ap()
    oT = nc.dram_tensor("oT", [D, TOK], F32, kind="ExternalOutput").ap()

    stack = ExitStack()
    sb = lambda name, shape, dt: stack.enter_context(nc.sbuf_tensor(name, shape, dt))
    xbuf = sb("xbuf", [128, 16, T], F32)
    hbuf = sb("hbuf", [128, 16, T], BF16)
    ubuf = sb("ubuf", [128, 16, T], BF16)
    wring = sb("wring", [128, RING, SLOT], BF16)
    cst = sb("cstsb", [128, NCST], F32)
    ones = sb("ones", [128, 128], BF16)
    KT = sb("KT", [128, 4, NMEM], BF16)
    Vb = sb("Vb", [128, 2, 512], BF16)
    pwb = sb("pwb", [128, POOLW_SIZE], BF16)
    NSF = 10
    NSB = 6
    SW = 376
    SWB = 344
    scf = sb("scf", [128, NSF, SW], F32)
    scb = sb("scb", [128, NSB, SW], BF16)
    qb = sb("qb", [128, 3, SWB], BF16)
    eb = sb("eb", [128, 4, SWB], BF16)
    pb = sb("pb", [128, 6, SWB], BF16)
    acc = sb("acc", [128, 3, SWB], F32)
    ones_f = sb("ones_f", [128, 128], F32)
    psum = [stack.enter_context(nc.psum_tensor("ps%d" % b, [128, 512], F32)) for b in range(8)]

    P = Prog()
    cnt = {"f": 0, "b": 0, "slab": 0, "q": 0, "e": 0, "p": 0, "proj": 0, "u": 0, "d": 0}

    def rot(name, n):
        i = cnt[name] % n
        cnt[name] += 1
        return i

    new_f = lambda: rot("f", NSF)
    new_b = lambda: rot("b", NSB)
    proj_bank = lambda: rot("proj", 4)

    def cc(col):
        return cst[:, col:col + 1]

    def pw(pc, oc):
        i = PW_TILES.index((pc, oc))
        return pwb[:, i * 128:(i + 1) * 128]

    XB = [[("x", c, bi) for c in range(16)] for bi in range(3)]

    slab_reads = {}

    def load_slab(key):
        n = cnt["slab"]; cnt["slab"] += 1
        slot = n % RING
        size = _slab_size(key)
        off = W_OFF[key]
        P.op("gq", lambda e, slot=slot, size=size, off=off: e.dma_start(out=wring[:, slot, 0:size], in_=wst[:, off:off + size]),
             reads=slab_reads.get(n, []), writes=[("wr", slot)], chain=("wr", slot))
        return slot

    def mm_group(bank, ncols, pairs, reads):
        def fn(e, bank=bank, ncols=ncols, pairs=pairs):
            n = len(pairs)
            return [e.matmul(psum[bank][0:mrows, 0:ncols], lhsT=l, rhs=r, start=(i == 0), stop=(i == n - 1))
                    for i, (l, r, mrows) in enumerate(pairs)]
        P.op("pe", fn, reads=reads, writes=[("ps", bank)])

    def hres(t):
        r = [("h", k, t) for k in range(16)]
        if t > 0:
            r += [("h", k, t - 1) for k in range(16)]
        return r

    HALL = [("h", k, t) for k in range(16) for t in range(3)]

    def proj(slot, rows, lo, hi, t):
        bank = proj_bank()
        mm_group(bank, hi - lo, [(wring[:, slot, k * rows:(k + 1) * rows], hbuf[:, k, lo:hi], rows) for k in range(16)],
                 reads=[("wr", slot)] + hres(t))
        return bank

    def sq_accum(c, blk, bank, bi):
        s, e_ = blk
        w = e_ - s
        sq = new_b()
        P.op("act", lambda e, c=c, s=s, e_=e_, w=w, sq=sq: e.activation(out=scb[:, sq, 0:w], in_=xbuf[:, c, s:e_], func=AF.Square),
             reads=[("x", c, bi)], writes=[("scb", sq)])

        def pe_part():
            P.op("pe", lambda e, c=c, w=w, sq=sq, bank=bank: e.matmul(psum[bank][:, 0:w], lhsT=ones[:, :], rhs=scb[:, sq, 0:w], start=(c == 0), stop=(c == 15)),
                 reads=[("scb", sq), ("ones",)], writes=[("ps", bank)])
        return pe_part

    def sq_acc_dve(c, blk, t):
        s, e_ = blk
        w = e_ - s
        sq = new_b()
        P.op("act", lambda e, c=c, s=s, e_=e_, w=w, sq=sq: e.activation(out=scb[:, sq, 0:w], in_=xbuf[:, c, s:e_], func=AF.Square),
             reads=[("x", c, t)], writes=[("scb", sq)])

        def dve_part():
            if c == 0:
                P.op("dve", lambda e, w=w, sq=sq, t=t: e.tensor_copy(out=acc[:, t, 0:w], in_=scb[:, sq, 0:w]),
                     reads=[("scb", sq)], writes=[("acc", t)])
            else:
                P.op("dve", lambda e, w=w, sq=sq, t=t: e.tensor_tensor(out=acc[:, t, 0:w], in0=acc[:, t, 0:w], in1=scb[:, sq, 0:w], op=ALU.add),
                     reads=[("scb", sq), ("acc", t)], writes=[("acc", t)])
        return dve_part

    def stats_finish(blk, t, bank):
        w = blk[1] - blk[0]
        P.op("pe", lambda e, w=w, t=t, bank=bank: e.matmul(psum[bank][:, 0:w], lhsT=ones_f[:, :], rhs=acc[:, t, 0:w], start=True, stop=True),
             reads=[("acc", t), ("onesf",)], writes=[("ps", bank)])

    def rstd_of(blk, bank):
        s, e_ = blk
        w = e_ - s
        f = new_f()
        P.op("act", lambda e, w=w, f=f, bank=bank: e.activation(out=scf[:, f, 0:w], in_=psum[bank][:, 0:w], func=AF.Ln, bias=cc(C_EPS), scale=1.0 / D),
             reads=[("ps", bank), ("cst",)], writes=[("scf", f)])
        r = new_f()
        P.op("act", lambda e, w=w, f=f, r=r: e.activation(out=scf[:, r, 0:w], in_=scf[:, f, 0:w], func=AF.Exp, scale=-0.5),
             reads=[("scf", f)], writes=[("scf", r)])
        return r

    def norm_apply(c, blk, bi, r, gcol, to_h):
        s, e_ = blk
        w = e_ - s
        if to_h:
            P.op("dve", lambda e, c=c, s=s, e_=e_, w=w, r=r: e.scalar_tensor_tensor(out=hbuf[:, c, s:e_], in0=xbuf[:, c, s:e_], scalar=cc(gcol + c), in1=scf[:, r, 0:w], op0=ALU.mult, op1=ALU.mult),
                 reads=[("x", c, bi), ("scf", r), ("cst",)], writes=[("h", c, bi)])
        else:
            P.op("dve", lambda e, c=c, s=s, e_=e_, w=w, r=r: e.scalar_tensor_tensor(out=xbuf[:, c, s:e_], in0=xbuf[:, c, s:e_], scalar=cc(gcol + c), in1=scf[:, r, 0:w], op0=ALU.mult, op1=ALU.mult),
                 reads=[("x", c, bi), ("scf", r), ("cst",)], writes=[("x", c, bi)])

    NBANK = (4, 5, 6)
    NBANK_NEXT = (3, 4, 5)

    P.op("sp", lambda e: e.dma_start(out=cst[:, :], in_=cstd[:, :]), writes=[("cst",)], chain=("cst",))
    P.op("dve", lambda e: e.memset(ones[:, :], 1.0), writes=[("ones",)])
    P.op("dve", lambda e: e.memset(ones_f[:, :], 1.0), writes=[("onesf",)])
    P.op("gq", lambda e: e.dma_start(out=pwb[:, :], in_=wst[:, W_OFF["poolw"]:W_OFF["poolw"] + POOLW_SIZE]),
         writes=[("pwb",)], chain=("pwb",))
    memv = memT.rearrange("(c p) m -> c p m", p=128)
    xv = xT.rearrange("(c p) t -> c p t", p=128)
    ov = oT.rearrange("(c p) t -> c p t", p=128)
    mT = ubuf[:, 12:16, :].rearrange("p a b -> p (a b)")

    xv4 = xT.rearrange("(c p) t -> p c t", p=128)
    ov4 = oT.rearrange("(c p) t -> p c t", p=128)

    def load_x(job_, bi, cg, extra=()):
        c0_ = job_ * JNEW
        s, e_ = NB1[bi]
        P.op("sp", lambda e, cg=cg, c0_=c0_, s=s, e_=e_: e.dma_start(out=xbuf[:, 4 * cg:4 * cg + 4, s:e_], in_=xv4[:, 4 * cg:4 * cg + 4, c0_ + s:c0_ + e_]),
             reads=list(extra), writes=[("x", c, bi) for c in range(4 * cg, 4 * cg + 4)], chain=("x", cg, bi))

    def store_o(job_, t, cg):
        c0_ = job_ * JNEW
        a0, b0 = FT[t]
        P.op("sp", lambda e, cg=cg, c0_=c0_, a0=a0, b0=b0: e.dma_start(out=ov4[:, 4 * cg:4 * cg + 4, c0_ + a0 - HALO:c0_ + b0 - HALO], in_=xbuf[:, 4 * cg:4 * cg + 4, a0:b0]),
             reads=[("x", c, t) for c in range(4 * cg, 4 * cg + 4)], chain=("o", cg, t))

    for cg in range(4):
        load_x(0, 0, cg)
    slab_reads[0] = XB[0]
    slab_reads[6] = XB[2]

    kvq = []
    KVLAG = 2
    rmk = acc[:, 0, 0:NMEM]

    def kv_pieces():
        MBANK = 7
        st1 = {}
        st2 = {}

        def dma(c, st):
            f = new_f()
            P.op("sp", lambda e, c=c, f=f: e.dma_start(out=scf[:, f, 0:NMEM], in_=memv[c]), writes=[("scf", f)], chain=("scfd", f))
            st[c] = f

        def p1(c):
            if c == 0:
                for c2 in range(KVLAG):
                    dma(c2, st1)
            if c + KVLAG < 16:
                dma(c + KVLAG, st1)
            f = st1[c]
            sq = new_b()
            P.op("act", lambda e, f=f, sq=sq: e.activation(out=scb[:, sq, 0:NMEM], in_=scf[:, f, 0:NMEM], func=AF.Square),
                 reads=[("scf", f)], writes=[("scb", sq)])
            P.op("pe", lambda e, c=c, sq=sq: e.matmul(psum[MBANK][:, 0:NMEM], lhsT=ones[:, :], rhs=scb[:, sq, 0:NMEM], start=(c == 0), stop=(c == 15)),
                 reads=[("scb", sq), ("ones",)], writes=[("ps", MBANK)])

        def pr():
            for c2 in range(KVLAG):
                dma(c2, st2)
            rm = rstd_of((0, NMEM), MBANK)
            P.op("act", lambda e, rm=rm: e.activation(out=rmk, in_=scf[:, rm, 0:NMEM], func=AF.Copy), reads=[("scf", rm)], writes=[("acc", 0)])

        def p2(c):
            if c + KVLAG < 16:
                dma(c + KVLAG, st2)
            f = st2[c]
            P.op("dve", lambda e, c=c, f=f: e.scalar_tensor_tensor(out=mT[:, c * NMEM:(c + 1) * NMEM], in0=scf[:, f, 0:NMEM], scalar=cc(C_GMEM + c), in1=rmk, op0=ALU.mult, op1=ALU.mult),
                 reads=[("scf", f), ("acc", 0), ("cst",)], writes=[("u", 12), ("u", 13), ("u", 14), ("u", 15)])

        for c in range(16):
            kvq.append(lambda c=c: p1(c))
        kvq.append(pr)
        for c in range(16):
            kvq.append(lambda c=c: p2(c))

    def kv_pop(n=2):
        for _ in range(n):
            if kvq:
                kvq.pop(0)()

    def kv_prologue():
        while kvq:
            kvq.pop(0)()
        MTR = [("u", 12), ("u", 13), ("u", 14), ("u", 15)]
        for h in range(4):
            slot = load_slab(("kh", h))
            bank = proj_bank()
            mm_group(bank, NMEM, [(wring[:, slot, k * 128:(k + 1) * 128], mT[:, k * NMEM:(k + 1) * NMEM], 128) for k in range(16)],
                     reads=[("wr", slot)] + MTR)
            P.op("act", lambda e, h=h, bank=bank: e.activation(out=KT[:, h, :], in_=psum[bank][:, 0:NMEM], func=AF.Copy),
                 reads=[("ps", bank)], writes=[("KT",)])
        for h in range(4):
            slot = load_slab(("vh", h))
            for mc in range(2):
                bank = proj_bank()
                mm_group(bank, 128, [(mT[:, k * NMEM + mc * 128:k * NMEM + (mc + 1) * 128], wring[:, slot, k * 128:(k + 1) * 128], 128) for k in range(16)],
                         reads=[("wr", slot)] + MTR)
                P.op("act", lambda e, h=h, mc=mc, bank=bank: e.activation(out=Vb[:, mc, h * 128:(h + 1) * 128], in_=psum[bank][:, 0:128], func=AF.Copy),
                     reads=[("ps", bank)], writes=[("Vb",)])

    def stats_block(bi):
        for c in range(16):
            sq_accum(c, NB1[bi], NBANK[bi], bi)()

    def norm1_block(bi):
        r_ = rstd_of(NB1[bi], NBANK[bi])
        for c in range(16):
            norm_apply(c, NB1[bi], bi, r_, C_GMIX, True)

    for job in range(NJOB):
        c0 = job * JNEW

        def att_head(h):
            slot = load_slab(("q", h))
            qs = {}
            es = {}

            def U(t):
                m0, b0 = MT[t]
                w = b0 - m0
                bank = proj(slot, 128, m0, b0, t)
                q = rot("q", 3)
                P.op("act", lambda e, w=w, q=q, bank=bank: e.activation(out=qb[:, q, 0:w], in_=psum[bank][:, 0:w], func=AF.Copy),
                     reads=[("ps", bank)], writes=[("qb", q)])
                qs[t] = q

            def S(t):
                m0, b0 = MT[t]
                w = b0 - m0
                et = []
                for mc in range(2):
                    bank = 4 + 2 * (t % 2) + mc
                    mm_group(bank, w, [(KT[:, h, mc * 128:(mc + 1) * 128], qb[:, qs[t], 0:w], 128)],
                             reads=[("KT",), ("qb", qs[t])])
                    ei = rot("e", 4)
                    P.op("act", lambda e, w=w, ei=ei, bank=bank: e.activation(out=eb[:, ei, 0:w], in_=psum[bank][:, 0:w], func=AF.Exp, scale=float(ATT_SCALE)),
                         reads=[("ps", bank)], writes=[("eb", ei)])
                    et.append(ei)
                es[t] = et

            def A(t):
                m0, b0 = MT[t]
                w = b0 - m0
                ba = proj_bank()
                bs = proj_bank()
                mm_group(ba, w, [(Vb[:, mc, h * 128:(h + 1) * 128], eb[:, es[t][mc], 0:w], 128) for mc in range(2)],
                         reads=[("Vb",)] + [("eb", es[t][mc]) for mc in range(2)])
                mm_group(bs, w, [(ones[:, :], eb[:, es[t][mc], 0:w], 128) for mc in range(2)],
                         reads=[("ones",)] + [("eb", es[t][mc]) for mc in range(2)])
                fl = new_f()
                P.op("act", lambda e, w=w, fl=fl, bs=bs: e.activation(out=scf[:, fl, 0:w], in_=psum[bs][:, 0:w], func=AF.Ln),
                     reads=[("ps", bs)], writes=[("scf", fl)])
                f = new_f()
                P.op("act", lambda e, w=w, fl=fl, f=f: e.activation(out=scf[:, f, 0:w], in_=scf[:, fl, 0:w], func=AF.Exp, scale=-1.0),
                     reads=[("scf", fl)], writes=[("scf", f)])
                P.op("dve", lambda e, w=w, f=f, ba=ba, m0=m0, b0=b0: e.tensor_tensor(out=ubuf[:, 12 + h, m0:b0], in0=psum[ba][:, 0:w], in1=scf[:, f, 0:w], op=ALU.mult),
                     reads=[("ps", ba), ("scf", f)], writes=[("u", 12 + h)])

            U(0); U(1); S(0); U(2); S(1); A(0); S(2); A(1); A(2)

        lin_q = []

        PC_WIN = {0: (2, 2), 1: (2, 4), 2: (4, 4), 3: (8, 8), 4: (8, 16), 5: (16, 16)}

        def pool_step(pc, t, slot):
            klo, khi = PC_WIN[pc]
            m0, b0 = MT[t]
            w = b0 - m0
            wn = w + 15
            bank = proj(slot, 128, m0 - 15, b0, t)
            v = new_f()
            P.op("act", lambda e, wn=wn, v=v, bank=bank: e.activation(out=scf[:, v, 0:wn], in_=psum[bank][:, 0:wn], func=AF.Copy),
                 reads=[("ps", bank)], writes=[("scf", v)])
            sums = {1: v}
            cur = v
            sh = 1
            while sh < khi:
                nx = new_f()
                P.op("dve", lambda e, wn=wn, cur=cur, nx=nx, sh=sh: e.tensor_tensor(out=scf[:, nx, sh:wn], in0=scf[:, cur, sh:wn], in1=scf[:, cur, 0:wn - sh], op=ALU.add),
                     reads=[("scf", cur)], writes=[("scf", nx)])
                cur = nx
                sh *= 2
                sums[sh] = cur
            pi = rot("p", 6)
            halves = [(0, 128, klo)] if klo == khi else [(0, 64, klo), (64, 128, khi)]
            for (r0, r1, kk) in halves:
                sk = sums[kk]
                wi = POOL_K.index(kk)
                if job == 0 and t == 0:
                    P.op("dve", lambda e, r0=r0, r1=r1, sk=sk, wi=wi: e.tensor_tensor(out=scf[r0:r1, sk, 17:33], in0=scf[r0:r1, sk, 17:33], in1=cst[r0:r1, C_CORR + 16 * wi:C_CORR + 16 * wi + 16], op=ALU.mult),
                         reads=[("scf", sk), ("cst",)], writes=[("scf", sk)])
                P.op("dve", lambda e, r0=r0, r1=r1, w=w, sk=sk, v=v, pi=pi, kk=kk: e.scalar_tensor_tensor(out=pb[r0:r1, pi, 0:w], in0=scf[r0:r1, sk, 15:15 + w], scalar=1.0 / kk, in1=scf[r0:r1, v, 15:15 + w], op0=ALU.mult, op1=ALU.subtract),
                     reads=[("scf", sk), ("scf", v)], writes=[("pb", pi)])
            return pi

        def pool_linear(blk, t, pooled):
            m0, b0 = MT[t]
            w = b0 - m0
            for oc in range(3 * blk, 3 * blk + 3):
                ks = [pc for (pc, o_) in PW_TILES if o_ == oc]
                bank = proj_bank()
                mm_group(bank, w, [(pw(pc, oc), pb[:, pooled[pc - 3 * blk], 0:w], 128) for pc in ks],
                         reads=[("pwb",)] + [("pb", pooled[pc - 3 * blk]) for pc in ks])
                P.op("act", lambda e, bank=bank, oc=oc, m0=m0, b0=b0, w=w: e.activation(out=ubuf[:, 6 + oc, m0:b0], in_=psum[bank][:, 0:w], func=AF.Identity, scale=cc(C_POOLSC + oc)),
                     reads=[("ps", bank), ("cst",)], writes=[("u", 6 + oc)])

        def conv_tile(i, t, scx, scc, scbk):
            m0, b0 = MT[t]
            w = b0 - m0
            wn = w + 2
            bx = proj(scx, 128, m0 - 2, b0, t)
            fx = new_f()
            P.op("act", lambda e, wn=wn, fx=fx, bx=bx: e.activation(out=scf[:, fx, 0:wn], in_=psum[bx][:, 0:wn], func=AF.Copy),
                 reads=[("ps", bx)], writes=[("scf", fx)])
            bc = proj(scc, 128, m0 - 2, b0, t)
            fv = new_f()
            P.op("dve", lambda e, wn=wn, fx=fx, fv=fv, bc=bc: e.tensor_tensor(out=scf[:, fv, 0:wn], in0=psum[bc][:, 0:wn], in1=scf[:, fx, 0:wn], op=ALU.mult),
                 reads=[("ps", bc), ("scf", fx)], writes=[("scf", fv)])
            fc = new_f()
            P.op("act", lambda e, w=w, fv=fv, fc=fc, i=i: e.activation(out=scf[:, fc, 0:w], in_=scf[:, fv, 2:2 + w], func=AF.Identity, scale=cc(C_CONVW + 3 * i + 2)),
                 reads=[("scf", fv), ("cst",)], writes=[("scf", fc)])
            bb = proj(scbk, 128, m0, b0, t)
            P.op("dve", lambda e, w=w, fv=fv, fc=fc, i=i: e.scalar_tensor_tensor(out=scf[:, fc, 0:w], in0=scf[:, fv, 1:1 + w], scalar=cc(C_CONVW + 3 * i + 1), in1=scf[:, fc, 0:w], op0=ALU.mult, op1=ALU.add),
                 reads=[("scf", fv), ("scf", fc), ("cst",)], writes=[("scf", fc)])
            fc2 = new_f()
            P.op("dve", lambda e, w=w, fv=fv, fc=fc, fc2=fc2, i=i: e.scalar_tensor_tensor(out=scf[:, fc2, 0:w], in0=scf[:, fv, 0:w], scalar=cc(C_CONVW + 3 * i + 0), in1=scf[:, fc, 0:w], op0=ALU.mult, op1=ALU.add),
                 reads=[("scf", fv), ("scf", fc), ("cst",)], writes=[("scf", fc2)])
            P.op("dve", lambda e, w=w, fc2=fc2, bb=bb, i=i, m0=m0, b0=b0: e.tensor_tensor(out=ubuf[:, i, m0:b0], in0=psum[bb][:, 0:w], in1=scf[:, fc2, 0:w], op=ALU.mult),
                 reads=[("ps", bb), ("scf", fc2)], writes=[("u", i)])

        def conv_head(i):
            scx = load_slab(("cx", i))
            scc = load_slab(("cc", i))
            scbk = load_slab(("cb", i))
            for t in range(3):
                conv_tile(i, t, scx, scc, scbk)
                kv_pop()
                if lin_q:
                    lin_q.pop(0)()

        if job == 0:
            stats_block(0)
            norm1_block(0)
            sl0 = [load_slab(("cx", 0)), load_slab(("cc", 0)), load_slab(("cb", 0))]
            sl1 = [load_slab(("cx", 1)), load_slab(("cc", 1)), load_slab(("cb", 1))]
            for bi in (1, 2):
                for cg in range(4):
                    load_x(0, bi, cg, extra=[("wr", sl1[2])])
        else:
            sl0 = [load_slab(("cx", 0)), load_slab(("cc", 0)), load_slab(("cb", 0))]
            sl1 = [load_slab(("cx", 1)), load_slab(("cc", 1)), load_slab(("cb", 1))]
        conv_tile(0, 0, *sl0)
        conv_tile(1, 0, *sl1)
        stats_block(1)
        norm1_block(1)
        stats_block(2)
        conv_tile(0, 1, *sl0)
        norm1_block(2)
        conv_tile(1, 1, *sl1)
        conv_tile(0, 2, *sl0)
        conv_tile(1, 2, *sl1)
        if job == 0:
            kv_pieces()
        conv_head(2)
        for blk in (1, 0):
            pslots = [load_slab(("pv", 3 * blk + i)) for i in range(3)]
            for t in range(3):
                pooled = []
                for i in range(3):
                    pooled.append(pool_step(3 * blk + i, t, pslots[i]))
                    kv_pop()
                if lin_q:
                    lin_q.pop(0)()
                lin_q.append(lambda blk=blk, t=t, pooled=pooled: pool_linear(blk, t, pooled))
        if job == 0:
            kv_prologue()
        for h in range(4):
            att_head(h)
            if lin_q:
                lin_q.pop(0)()
        for i in (3, 4, 5):
            conv_head(i)
        while lin_q:
            lin_q.pop(0)()

        def wout_group(m, t, s0):
            m0, b0 = MT[t]
            w = b0 - m0
            bank = proj_bank()
            prs = [(wring[:, s0, kt * 128:(kt + 1) * 128], ubuf[:, kt, m0:b0], 128) for kt in range(16)]
            mm_group(bank, w, prs, reads=[("wr", s0)] + [("u", kt) for kt in range(16)])
            P.op("dve", lambda e, w=w, m=m, bank=bank, m0=m0, b0=b0: e.tensor_tensor(out=xbuf[:, m, m0:b0], in0=psum[bank][:, 0:w], in1=xbuf[:, m, m0:b0], op=ALU.add),
                 reads=[("ps", bank), ("x", m, t)], writes=[("x", m, t)])
            return sq_acc_dve(m, MT[t], t)

        def norm2_block(bi):
            stats_finish(MT[bi], bi, NBANK[bi])
            r_ = rstd_of(MT[bi], NBANK[bi])
            for c in range(16):
                norm_apply(c, MT[bi], bi, r_, C_GFFN, True)
            if job == 0 and bi == 0:
                P.op("dve", lambda e: e.tensor_scalar(out=hbuf[:, :, 30:32], in0=hbuf[:, :, 30:32], scalar1=cc(C_HMASK), scalar2=None, op0=ALU.mult),
                     reads=[("h", k, 0) for k in range(16)] + [("cst",)], writes=[("h", k, 0) for k in range(16)])

        pend = []
        for m in range(14):
            s0 = load_slab(("wout", m))
            newp = [wout_group(m, t, s0) for t in range(3)]
            for p_ in pend:
                p_()
            pend = newp
        sA = load_slab(("wout", 14))
        sB = load_slab(("wout", 15))
        for t in range(3):
            for (m, s_) in ((14, sA), (15, sB)):
                newp = [wout_group(m, t, s_)]
                for p_ in pend:
                    p_()
                pend = newp
            for p_ in pend:
                p_()
            pend = []
            norm2_block(t)

        def up_pair(j):
            sg_ = load_slab(("upg", j))
            sv_ = load_slab(("upv", j))
            for t in range(3):
                up_unit(j, t, sg_, sv_)

        def up_unit(j, t, sg_, sv_):
            aslot = j % ARING
            a0, b0 = FT[t]
            if True:
                w = b0 - a0
                wn = w + 2
                u = rot("u", 3)
                bg, bv = 2 * u, 2 * u + 1
                def fn(e, bg=bg, bv=bv, wn=wn, sg_=sg_, sv_=sv_, a0=a0, b0=b0):
                    out_ = []
                    for bank_, sl_ in ((bg, sg_), (bv, sv_)):
                        for k in range(16):
                            out_.append(e.matmul(psum[bank_][:, 0:wn], lhsT=wring[:, sl_, k * 128:(k + 1) * 128], rhs=hbuf[:, k, a0 - 2:b0], start=(k == 0), stop=(k == 15)))
                    return out_
                P.op("pe", fn, reads=[("wr", sg_), ("wr", sv_)] + hres(t), writes=[("ps", bg), ("ps", bv)])
                tg = new_f(); tv = new_f(); sgb = new_f()
                wg = C_FFNW + 3 * j
                wv = C_FFNW + 3 * (NPAIR + j)
                P.op("act", lambda e, w=w, tg=tg, bg=bg, wg=wg, j=j: e.activation(out=scf[:, tg, 0:w], in_=psum[bg][:, 2:2 + w], func=AF.Identity, bias=cc(C_FFNB + j), scale=cc(wg + 2)),
                     reads=[("ps", bg), ("cst",)], writes=[("scf", tg)])
                P.op("act", lambda e, w=w, tv=tv, bv=bv, wv=wv, j=j: e.activation(out=scf[:, tv, 0:w], in_=psum[bv][:, 2:2 + w], func=AF.Identity, bias=cc(C_FFNB + NPAIR + j), scale=cc(wv + 2)),
                     reads=[("ps", bv), ("cst",)], writes=[("scf", tv)])
                P.op("dve", lambda e, w=w, tg=tg, bg=bg, wg=wg: e.scalar_tensor_tensor(out=scf[:, tg, 0:w], in0=psum[bg][:, 1:1 + w], scalar=cc(wg + 1), in1=scf[:, tg, 0:w], op0=ALU.mult, op1=ALU.add),
                     reads=[("ps", bg), ("scf", tg), ("cst",)], writes=[("scf", tg)])
                P.op("dve", lambda e, w=w, tv=tv, bv=bv, wv=wv: e.scalar_tensor_tensor(out=scf[:, tv, 0:w], in0=psum[bv][:, 1:1 + w], scalar=cc(wv + 1), in1=scf[:, tv, 0:w], op0=ALU.mult, op1=ALU.add),
                     reads=[("ps", bv), ("scf", tv), ("cst",)], writes=[("scf", tv)])
                P.op("dve", lambda e, w=w, tg=tg, bg=bg, wg=wg: e.scalar_tensor_tensor(out=scf[:, tg, 0:w], in0=psum[bg][:, 0:w], scalar=cc(wg), in1=scf[:, tg, 0:w], op0=ALU.mult, op1=ALU.add),
                     reads=[("ps", bg), ("scf", tg), ("cst",)], writes=[("scf", tg)])
                P.op("dve", lambda e, w=w, tv=tv, bv=bv, wv=wv: e.scalar_tensor_tensor(out=scf[:, tv, 0:w], in0=psum[bv][:, 0:w], scalar=cc(wv), in1=scf[:, tv, 0:w], op0=ALU.mult, op1=ALU.add),
                     reads=[("ps", bv), ("scf", tv), ("cst",)], writes=[("scf", tv)])
                P.op("act", lambda e, w=w, tg=tg, sgb=sgb: e.activation(out=scf[:, sgb, 0:w], in_=scf[:, tg, 0:w], func=AF.Silu),
                     reads=[("scf", tg)], writes=[("scf", sgb)])
                P.op("dve", lambda e, w=w, tv=tv, sgb=sgb, aslot=aslot, a0=a0, b0=b0: e.tensor_tensor(out=ubuf[:, aslot, a0:b0], in0=scf[:, sgb, 0:w], in1=scf[:, tv, 0:w], op=ALU.mult),
                     reads=[("scf", sgb), ("scf", tv)], writes=[("u", aslot)])

        FBANK = (0, 1, 2)

        def down_group(g):
            pend = []
            for m in range(16):
                slot = load_slab(("down", g, m))
                newp = []
                for t, (a0, b0) in enumerate(FT):
                    w = b0 - a0
                    bank = 6 + rot("d", 2)
                    prs = [(wring[:, slot, kk * 128:(kk + 1) * 128], ubuf[:, (g * GK + kk) % ARING, a0:b0], 128) for kk in range(GK)]
                    mm_group(bank, w, prs, reads=[("wr", slot)] + [("u", (g * GK + kk) % ARING) for kk in range(GK)])
                    P.op("dve", lambda e, w=w, m=m, bank=bank, a0=a0, b0=b0: e.tensor_tensor(out=xbuf[:, m, a0:b0], in0=psum[bank][:, 0:w], in1=xbuf[:, m, a0:b0], op=ALU.add),
                         reads=[("ps", bank), ("x", m, t)], writes=[("x", m, t)])
                    if g == NGRP - 1:
                        newp.append(sq_acc_dve(m, FT[t], t))
                for p_ in pend:
                    p_()
                pend = newp
            for p_ in pend:
                p_()

        ps0 = (load_slab(("upg", 0)), load_slab(("upv", 0)))
        ps1 = (load_slab(("upg", 1)), load_slab(("upv", 1)))
        for t in range(3):
            up_unit(0, t, *ps0)
            up_unit(1, t, *ps1)
        done = 2
        for g in range(NGRP):
            hi = min(NPAIR, GK * (g + 1) + 1)
            for j in range(done, hi):
                up_pair(j)
            done = hi
            down_group(g)

        nxt = job + 1 < NJOB
        for bi, blk in enumerate(FT):
            stats_finish(blk, bi, FBANK[bi])
        rs_ = [rstd_of(blk, FBANK[bi]) for bi, blk in enumerate(FT)]
        for t in range(3):
            if t == 2 and nxt:
                stats_block(0)
                norm1_block(0)
            for c in range(16):
                norm_apply(c, FT[t], t, rs_[t], C_GFIN, False)
                if c % 4 == 3:
                    cg = c // 4
                    store_o(job, t, cg)
                    if nxt and cg >= 1:
                        load_x(job + 1, t, cg - 1)
            if nxt:
                load_x(job + 1, t, 3)

    _emit(nc, P, stack, [("o", cg, t) for cg in range(4) for t in range(3)])
    stack.close()
    return nc


_NC_CACHE = {}


def kernel(x, mem, g_mix, g_mem, w_in, conv_w, pool_w, pool_scale, w_kv, w_out,
           g_ffn, w_up, ffn_conv_w, ffn_conv_b, w_down, g_final):
    f = lambda a: np.asarray(a, dtype=np.float32)
    x = f(x)[0]; mem = f(mem)[0]
    g_mix = f(g_mix)[0]; g_mem = f(g_mem)[0]; g_ffn = f(g_ffn)[0]; g_final = f(g_final)
    w_in = f(w_in)[0]; conv_w = f(conv_w)[0]; pool_w = f(pool_w)[0]; pool_scale = f(pool_scale)[0]
    w_kv = f(w_kv)[0]; w_out = f(w_out)[0]; w_up = f(w_up)[0]
    ffn_conv_w = f(ffn_conv_w)[0]; ffn_conv_b = f(ffn_conv_b)[0]; w_down = f(w_down)[0]

    ws = _build_wstream(w_in, pool_w, w_kv, w_out, w_up, w_down)
    memT = np.ascontiguousarray(mem.T)
    in_maps = []
    for c in range(NCORES):
        xt = np.zeros((D, XCOLS), np.float32)
        lo = c * TOK - HALO
        if lo < 0:
            xt[:, HALO:] = x[0:TOK].T
        else:
            xt[:, :] = x[lo:lo + XCOLS].T
        cst = _build_consts(c, g_mix, g_mem, g_ffn, g_final, conv_w, pool_scale, ffn_conv_w, ffn_conv_b)
        in_maps.append({"xT": xt, "memT": memT, "wst": ws, "cst": cst})
    if "nc" not in _NC_CACHE:
        _NC_CACHE["nc"] = build_nc()
    nc = _NC_CACHE["nc"]
    res = run_bass_kernel_spmd(nc, in_maps, core_ids=list(range(NCORES)))
    out = np.empty((1, SEQ, D), np.float32)
    for c in range(NCORES):
        out[0, c * TOK:(c + 1) * TOK, :] = res.results[c]["oT"].T
    return out
```
